# Optimizing a Trainium2 kernel written in Bass

```python
import math
import jax, jax.numpy as jnp
from jax import lax
import numpy as np

D_MODEL = 2048
BATCH = 4
SEQ = 4096
DEPTH = 2

N_A_LAYERS = DEPTH // 2
N_B_LAYERS = DEPTH - N_A_LAYERS
D_FF = 5632
PLE_DIM = 256
EPS = 1e-6
N_NORMS = 8
NEG = -1e30

A_GROUPS = ((128, 1), (512, 4), (2048, 16))
A_N_GROUPS = 3
A_HEADS = 8
A_HEAD_DIM = 128
A_OUT = A_HEADS * A_HEAD_DIM
A_QKV = 3 * A_N_GROUPS * A_HEADS * A_HEAD_DIM
BLOCK = 128

B_HEADS = 16
QK_NOPE = 128
QK_ROPE = 64
V_DIM = 128
Q_LORA = 512
KV_LORA = 512
ROPE_THETA = 10000.0

kernel_name = "yoco_dilated_mla_macaron_hybrid"


def rmsnorm(x, g):
    xf = x.astype(jnp.float32)
    y = xf * lax.rsqrt(jnp.mean(xf * xf, axis=-1, keepdims=True) + EPS)
    return (y * g.astype(jnp.float32)).astype(x.dtype)


def swiglu(x, wg, wu, wd):
    return (jax.nn.silu(x @ wg) * (x @ wu)) @ wd


def alibi_slopes(n):
    return jnp.asarray(2.0 ** (-8.0 * np.arange(1, n + 1) / n), dtype=jnp.float32)


def rope(x, pos):
    r = x.shape[-1]
    inv_freq = jnp.asarray(ROPE_THETA ** (-np.arange(0, r, 2) / r), dtype=jnp.float32)
    ang = pos.astype(jnp.float32)[..., None] * inv_freq
    ang = ang.reshape(ang.shape[:2] + (1,) * (x.ndim - 3) + (r // 2,))
    cos, sin = jnp.cos(ang), jnp.sin(ang)
    xf = x.astype(jnp.float32)
    x1, x2 = xf[..., : r // 2], xf[..., r // 2:]
    return jnp.concatenate([x1 * cos - x2 * sin, x2 * cos + x1 * sin], axis=-1).astype(x.dtype)


def dilated_window_attention(q, k, v, pos, slopes, window, dilation):
    B, S, H, Dh = q.shape
    L = S // dilation
    n_blk = -(-L // BLOCK)
    Lp = n_blk * BLOCK
    N = B * dilation
    sub_w = window // dilation

    def to_strided(t):
        t = t.reshape((B, L, dilation) + t.shape[2:])
        t = jnp.swapaxes(t, 1, 2).reshape((N, L) + t.shape[3:])
        return jnp.pad(t, [(0, 0), (0, Lp - L)] + [(0, 0)] * (t.ndim - 2))

    def key_blocks(t):
        cur = t.reshape((N, n_blk, BLOCK) + t.shape[2:])
        prev = jnp.pad(cur, [(0, 0), (1, 0)] + [(0, 0)] * (cur.ndim - 2))[:, :-1]
        return jnp.concatenate([prev, cur], axis=2)

    qs = to_strided(q).reshape(N, n_blk, BLOCK, H, Dh)
    ks = key_blocks(to_strided(k))
    vs = key_blocks(to_strided(v))
    ps = to_strided(pos)
    pq = ps.reshape(N, n_blk, BLOCK)
    pk = key_blocks(ps)

    s = jnp.einsum('nbqhd,nbkhd->nbhqk', qs, ks).astype(jnp.float32) * (Dh ** -0.5)
    dist = jnp.abs(pq[..., :, None] - pk[..., None, :]).astype(jnp.float32)
    s = s - slopes[:, None, None] * dist[:, :, None]
    qi = jnp.arange(BLOCK)[:, None]
    kj = jnp.arange(2 * BLOCK)[None, :]
    diff = BLOCK + qi - kj
    band = (diff >= 0) & (diff <= sub_w)
    exists = (jnp.arange(n_blk)[:, None, None] > 0) | (kj[None] >= BLOCK)
    mask = band[None] & exists
    s = jnp.where(mask[None, :, None], s, NEG)
    lse = jax.nn.logsumexp(s, axis=-1)
    prob = jnp.exp(s - lse[..., None])
    o = jnp.einsum('nbhqk,nbkhd->nbqhd', prob.astype(v.dtype), vs)

    o = o.reshape(N, Lp, H, Dh)[:, :L]
    o = jnp.swapaxes(o.reshape(B, dilation, L, H, Dh), 1, 2).reshape(B, S, H, Dh)
    lse = jnp.swapaxes(lse, 2, 3).reshape(N, Lp, H)[:, :L]
    lse = jnp.swapaxes(lse.reshape(B, dilation, L, H), 1, 2).reshape(B, S, H)
    return o, lse


def dilated_mixture_attention(hn, pos, w_qkv, w_o):
    B, S, _ = hn.shape
    qkv = (hn @ w_qkv).reshape(B, S, 3, A_N_GROUPS, A_HEADS, A_HEAD_DIM)
    slopes = alibi_slopes(A_N_GROUPS * A_HEADS).reshape(A_N_GROUPS, A_HEADS)
    outs, lses = [], []
    for g, (window, dilation) in enumerate(A_GROUPS):
        o, l = dilated_window_attention(qkv[:, :, 0, g], qkv[:, :, 1, g], qkv[:, :, 2, g],
                                        pos, slopes[g], window, dilation)
        outs.append(o)
        lses.append(l)
    wts = jax.nn.softmax(jnp.stack(lses, axis=0), axis=0)
    o = jnp.sum(wts[..., None] * jnp.stack(outs, axis=0).astype(jnp.float32), axis=0)
    return o.astype(hn.dtype).reshape(B, S, A_OUT) @ w_o


def shared_latent_kv(h, pos, kv_in_norm, w_dkv, kv_norm, w_ukv):
    B, S, _ = h.shape
    ckr = rmsnorm(h, kv_in_norm) @ w_dkv
    c_kv = rmsnorm(ckr[..., :KV_LORA], kv_norm)
    k_rope = rope(ckr[..., KV_LORA:], pos)
    kv = (c_kv @ w_ukv).reshape(B, S, B_HEADS, QK_NOPE + V_DIM)
    return (kv[..., :QK_NOPE], k_rope, kv[..., QK_NOPE:])


def latent_attention(hn, pos, shared, w_dq, q_norm, w_uq, w_o):
    k_nope, k_rope, v = shared
    B, S, _ = hn.shape
    c_q = rmsnorm(hn @ w_dq, q_norm)
    q = (c_q @ w_uq).reshape(B, S, B_HEADS, QK_NOPE + QK_ROPE)
    q_nope, q_rope = q[..., :QK_NOPE], rope(q[..., QK_NOPE:], pos)
    scale = (QK_NOPE + QK_ROPE) ** -0.5
    nb = S // BLOCK
    qn_b = jnp.moveaxis(q_nope.reshape(B, nb, BLOCK, B_HEADS, QK_NOPE), 1, 0)
    qr_b = jnp.moveaxis(q_rope.reshape(B, nb, BLOCK, B_HEADS, QK_ROPE), 1, 0)
    key_idx = jnp.arange(S)

    def one_block(args):
        qn, qr, blk = args
        s = (jnp.einsum('bqhd,bkhd->bhqk', qn, k_nope)
             + jnp.einsum('bqhr,bkr->bhqk', qr, k_rope)).astype(jnp.float32) * scale
        qi = blk * BLOCK + jnp.arange(BLOCK)
        s = jnp.where(key_idx[None, :] <= qi[:, None], s, NEG)
        prob = jax.nn.softmax(s, axis=-1)
        return jnp.einsum('bhqk,bkhd->bqhd', prob.astype(v.dtype), v)

    o = lax.map(one_block, (qn_b, qr_b, jnp.arange(nb)))
    o = jnp.moveaxis(o, 0, 1).reshape(B, S, B_HEADS * V_DIM)
    return o @ w_o


def setup_inputs(seed: int = 0) -> dict:
    key = jax.random.key(seed)
    ks = jax.random.split(key, 24)
    f32 = jnp.float32

    def w(k, shape, fan_in):
        return jax.random.normal(k, shape, f32) * (fan_in ** -0.5)

    def gain(k, shape):
        return 1.0 + 0.02 * jax.random.normal(k, shape, f32)

    x = jax.random.normal(ks[0], (BATCH, SEQ, D_MODEL), f32)
    p = jax.random.normal(ks[1], (DEPTH, BATCH, SEQ, PLE_DIM), f32)
    offset = jax.random.randint(ks[2], (BATCH, 1), 0, 1024, dtype=jnp.int32)
    positions = (jnp.arange(SEQ, dtype=jnp.int32)[None, :] + offset).astype(jnp.int32)
    return {
        "x": x,
        "p": p,
        "positions": positions,
        "norms": gain(ks[3], (DEPTH, N_NORMS, D_MODEL)),
        "ffn1_wg": w(ks[4], (DEPTH, D_MODEL, D_FF), D_MODEL),
        "ffn1_wu": w(ks[5], (DEPTH, D_MODEL, D_FF), D_MODEL),
        "ffn1_wd": w(ks[6], (DEPTH, D_FF, D_MODEL), D_FF),
        "ffn2_wg": w(ks[7], (DEPTH, D_MODEL, D_FF), D_MODEL),
        "ffn2_wu": w(ks[8], (DEPTH, D_MODEL, D_FF), D_MODEL),
        "ffn2_wd": w(ks[9], (DEPTH, D_FF, D_MODEL), D_FF),
        "ple_proj": w(ks[10], (DEPTH, PLE_DIM, D_MODEL), PLE_DIM),
        "ple_gate": w(ks[11], (DEPTH, D_MODEL, D_MODEL), D_MODEL),
        "a_wqkv": w(ks[12], (N_A_LAYERS, D_MODEL, A_QKV), D_MODEL),
        "a_wo": w(ks[13], (N_A_LAYERS, A_OUT, D_MODEL), A_OUT),
        "b_wdq": w(ks[14], (N_B_LAYERS, D_MODEL, Q_LORA), D_MODEL),
        "b_q_norm": gain(ks[15], (N_B_LAYERS, Q_LORA)),
        "b_wuq": w(ks[16], (N_B_LAYERS, Q_LORA, B_HEADS * (QK_NOPE + QK_ROPE)), Q_LORA),
        "b_wo": w(ks[17], (N_B_LAYERS, B_HEADS * V_DIM, D_MODEL), B_HEADS * V_DIM),
        "kv_in_norm": gain(ks[18], (D_MODEL,)),
        "w_dkv": w(ks[19], (D_MODEL, KV_LORA + QK_ROPE), D_MODEL),
        "kv_norm": gain(ks[20], (KV_LORA,)),
        "w_ukv": w(ks[21], (KV_LORA, B_HEADS * (QK_NOPE + V_DIM)), KV_LORA),
    }


def reference(x, p, positions, norms, ffn1_wg, ffn1_wu, ffn1_wd, ffn2_wg, ffn2_wu, ffn2_wd,
              ple_proj, ple_gate, a_wqkv, a_wo, b_wdq, b_q_norm, b_wuq, b_wo,
              kv_in_norm, w_dkv, kv_norm, w_ukv):
    h = x
    shared = None
    for i in range(DEPTH):
        g = norms[i]
        h = h + 0.5 * rmsnorm(swiglu(rmsnorm(h, g[0]), ffn1_wg[i], ffn1_wu[i], ffn1_wd[i]), g[1])
        hn = rmsnorm(h, g[2])
        if i < N_A_LAYERS:
            m = dilated_mixture_attention(hn, positions, a_wqkv[i], a_wo[i])
        else:
            j = i - N_A_LAYERS
            m = latent_attention(hn, positions, shared, b_wdq[j], b_q_norm[j], b_wuq[j], b_wo[j])
        h = h + rmsnorm(m, g[3])
        h = h + 0.5 * rmsnorm(swiglu(rmsnorm(h, g[4]), ffn2_wg[i], ffn2_wu[i], ffn2_wd[i]), g[5])
        gate = jax.nn.sigmoid(rmsnorm(h, g[6]) @ ple_gate[i])
        h = h + rmsnorm((p[i] @ ple_proj[i]) * gate, g[7])
        if i == N_A_LAYERS - 1:
            shared = shared_latent_kv(h, positions, kv_in_norm, w_dkv, kv_norm, w_ukv)
    return h
```

```python
import numpy as np
from contextlib import ExitStack
import concourse.bass as bass
import concourse.mybir as mybir
from concourse.bass_utils import run_bass_kernel_spmd

F32 = mybir.dt.float32
BF16 = mybir.dt.bfloat16
I32 = mybir.dt.int32
AF = mybir.ActivationFunctionType
ALU = mybir.AluOpType
AX = mybir.AxisListType

D = 2048
KC = 16
DFF = 5632
FC = 44
SEQ = 4096
HALF = 2048
EPS = 1e-6
NEGB = -30000.0


_UID = [0]


def _sb(nc, name, shape, dtype):
    _UID[0] += 1
    return nc.sbuf_tensor("%s_%d" % (name, _UID[0]), shape, dtype)


def _pst(nc, name, shape, dtype):
    _UID[0] += 1
    return nc.psum_tensor("%s_%d" % (name, _UID[0]), shape, dtype)


class Sched:
    NDMA = 8

    def __init__(self, nc, stack):
        self.nc = nc
        self.esem = {}
        for e in ("pe", "act", "dve", "pool"):
            self.esem[e] = stack.enter_context(nc.semaphore("s_" + e))
        self.dsem = {}
        for q in ("sp", "poolq"):
            self.dsem[q] = [stack.enter_context(nc.semaphore("d_%s%d" % (q, i))) for i in range(self.NDMA)]
        self.ecount = {e: 0 for e in self.esem}
        self.dcount = {q: 0 for q in self.dsem}
        self.waited = {}
        self.ops = []

    def op(self, eng, fn, reads=(), writes=()):
        writes = tuple(writes) + tuple(r for r in reads if isinstance(r, tuple) and r and r[0] == "ps")
        self.ops.append((eng, fn, tuple(reads), writes, False))

    def dma(self, q, fn, reads=(), writes=()):
        self.ops.append((q, fn, tuple(reads), tuple(writes), True))

    def flush(self):
        nc = self.nc
        ops = self.ops
        self.ops = []
        n = len(ops)
        deps = [None] * n
        signal = [False] * n
        last_writer = {}
        readers = {}
        dma_hist = {q: [] for q in self.dsem}
        for i, (eng, fn, rds, wrs, isdma) in enumerate(ops):
            d = set()
            for r in rds:
                w = last_writer.get(r)
                if w is not None:
                    d.add(w)
            for w_ in wrs:
                w = last_writer.get(w_)
                if w is not None:
                    d.add(w)
                rd = readers.get(w_)
                if rd:
                    d.update(rd.values())
            if isdma:
                hist = dma_hist[eng]
                if len(hist) >= self.NDMA:
                    d.add(hist[-self.NDMA])
                hist.append(i)
            d.discard(i)
            if eng == "pe":
                d = {x for x in d if ops[x][0] != "pe"}
            deps[i] = d
            for x in d:
                signal[x] = True
            for r in rds:
                rd = readers.setdefault(r, {})
                rd[("dma", i) if isdma else eng] = i
            for w_ in wrs:
                last_writer[w_] = i
                readers[w_] = {}
        info = [None] * n
        for i, (eng, fn, rds, wrs, isdma) in enumerate(ops):
            if isdma:
                k = self.dcount[eng]
                self.dcount[eng] += 1
                info[i] = (self.dsem[eng][k % self.NDMA], 16 * (k // self.NDMA + 1), ("d", eng, k % self.NDMA))
            elif signal[i]:
                self.ecount[eng] += 1
                info[i] = (self.esem[eng], self.ecount[eng], ("e", eng))
        engs = {"pe": "tensor", "act": "scalar", "dve": "vector", "pool": "gpsimd", "sp": "sync", "poolq": "gpsimd"}
        streams = {"tensor": [], "scalar": [], "vector": [], "gpsimd": [], "sync": []}
        for i, o in enumerate(ops):
            streams[engs[o[0]]].append(i)
        end_waits = []
        for q in self.dsem:
            k = self.dcount[q]
            for s in range(self.NDMA):
                cnt = (k - s + self.NDMA - 1) // self.NDMA if k > s else 0
                if cnt > 0:
                    end_waits.append((self.dsem[q][s], 16 * cnt, ("d", q, s)))

        def emit_stream(sname):
            def body(e):
                wt = self.waited.setdefault(sname, {})
                for i in streams[sname]:
                    eng, fn, rds, wrs, isdma = ops[i]
                    need = {}
                    for x in deps[i]:
                        s, val, key = info[x]
                        if key not in need or need[key][1] < val:
                            need[key] = (s, val)
                    for key, (s, val) in need.items():
                        if wt.get(key, 0) >= val:
                            continue
                        e.wait_ge(s, val)
                        wt[key] = val
                    ins = fn(e)
                    if isdma:
                        ins.then_inc(info[i][0], 16)
                    elif signal[i]:
                        ins.then_inc(info[i][0], 1)
                if sname == "sync":
                    for s, v, key in end_waits:
                        if wt.get(key, 0) < v:
                            e.wait_ge(s, v)
                            wt[key] = v
            return body

        with nc.Block() as block:
            block.tensor(emit_stream("tensor"))
            block.scalar(emit_stream("scalar"))
            block.vector(emit_stream("vector"))
            block.gpsimd(emit_stream("gpsimd"))
            block.sync(emit_stream("sync"))
        return n


def _mm(S, out, lhsT, rhs, start, stop, reads, writes):
    S.op("pe", lambda e, a=out, l=lhsT, r=rhs, st=start, sp=stop: e.matmul(a, l, r, start=st, stop=sp),
         reads, writes)


def _act(S, out, in_, func, reads, writes, bias=None, scale=None, accum_out=None):
    kw = {}
    if bias is not None:
        kw["bias"] = bias
    if scale is not None:
        kw["scale"] = scale
    if accum_out is not None:
        kw["accum_out"] = accum_out
    S.op("act", lambda e, o=out, i=in_, f=func, k=kw: e.activation(o, i, f, **k), reads, writes)


def _rstd_from_ss(S, rstd, ss_ps, nfeat, mult, reads, writes, eps_ap=None):
    m2 = 1.0 / (mult * mult)
    S.op("dve", lambda e, o=rstd, i=ss_ps: e.tensor_scalar(o, i, m2 / nfeat, EPS * m2, ALU.mult, ALU.add),
         reads, writes)
    S.op("act", lambda e, o=rstd: e.activation(o, o, AF.Sqrt), writes, writes)
    S.op("dve", lambda e, o=rstd: e.reciprocal(o, o), writes, writes)


class FFNTiles:
    def __init__(self, nc, stack):
        self.AT = stack.enter_context(_sb(nc, "ffn_AT", [128, FC, 1024], BF16))
        self.R = stack.enter_context(_sb(nc, "ffn_R", [128, 32768], BF16))
        self.wb = [stack.enter_context(_sb(nc, "ffn_wb%d" % i, [128, 5632], BF16)) for i in range(2)]
        self.sq = [stack.enter_context(_sb(nc, "ffn_sq%d" % i, [128, 512], BF16)) for i in range(2)]
        self.sg = [stack.enter_context(_sb(nc, "ffn_sg%d" % i, [128, 512], BF16)) for i in range(2)]
        self.hr = [stack.enter_context(_sb(nc, "ffn_hr%d" % i, [128, 512], F32)) for i in range(6)]
        self.tmp = [stack.enter_context(_sb(nc, "ffn_tmp%d" % i, [128, 512], F32)) for i in range(4)]
        self.rstd = [stack.enter_context(_sb(nc, "ffn_rstd%d" % i, [128, 512], F32)) for i in range(2)]
        self.rstdp = [stack.enter_context(_sb(nc, "ffn_rstdp%d" % i, [128, 512], F32)) for i in range(2)]
        self.ps = [stack.enter_context(_pst(nc, "ffn_ps%d" % i, [128, 512], F32)) for i in range(8)]
        self.Ysb = self.R[:, 0:16384].rearrange("p (c t) -> p c t", c=16)
        self.xnT = self.R[:, 16384:32768].rearrange("p (c t) -> p c t", c=16)


def ffn_block(nc, S, Tl, C, h_in, h_out, ntok, wgu_t, wd_t, gpre, gpost):
    ones = C["ones_bf"]
    psG = [Tl.ps[0], Tl.ps[2]]
    psU = [Tl.ps[1], Tl.ps[3]]
    psY = [Tl.ps[4], Tl.ps[5]]
    psS = [Tl.ps[6], Tl.ps[7]]
    st_ = {"sq": 0, "hr": 0, "tm": 0}

    def hload(c, ts):
        slot = st_["hr"] % 6
        st_["hr"] += 1
        S.dma("sp", lambda e, o=Tl.hr[slot][:], i=h_in[c, :, ts:ts + 512]: e.dma_start(out=o, in_=i),
              reads=[("dram", "h")], writes=[("hr", slot)])
        return slot

    def prenorm_gen(t0):
        for tb in range(2):
            ts = t0 + tb * 512
            pend = []
            for c in range(16 + 4):
                if c < 16:
                    pend.append((c, hload(c, ts)))
                if c >= 4:
                    cc, slot = pend.pop(0)
                    k = st_["sq"] % 2
                    st_["sq"] += 1
                    _act(S, Tl.sq[k][:], Tl.hr[slot][:], AF.Square, reads=[("hr", slot)], writes=[("sq", k)])
                    _mm(S, Tl.ps[tb][:], ones[:], Tl.sq[k][:], cc == 0, cc == 15, reads=[("sq", k)], writes=[("ps", tb)])
                    yield
            _rstd_from_ss(S, Tl.rstdp[tb][:], Tl.ps[tb][:], float(D), 1.0, reads=[("ps", tb)], writes=[("rstdp", tb)])
            yield
        for tb in range(2):
            ts = t0 + tb * 512
            pend = []
            for c in range(16 + 4):
                if c < 16:
                    pend.append((c, hload(c, ts)))
                if c >= 4:
                    cc, slot = pend.pop(0)
                    S.op("dve", lambda e, o=Tl.xnT[:, cc, tb * 512:(tb + 1) * 512], i=Tl.hr[slot][:], g=gpre[:, cc:cc + 1],
                         r=Tl.rstdp[tb][:]: e.scalar_tensor_tensor(o, i, g, r, ALU.mult, ALU.mult),
                         reads=[("hr", slot), ("rstdp", tb)], writes=[("xn", cc, tb)])
                    yield

    def post_gen(t0):
        for tb in range(2):
            ts = t0 + tb * 512
            _rstd_from_ss(S, Tl.rstd[tb][:], psS[tb][:], float(D), 0.5, reads=[("ps", 6 + tb)], writes=[("rstd", tb)])
            yield
            pend = []
            for c in range(16 + 4):
                if c < 16:
                    pend.append((c, hload(c, ts)))
                if c >= 4:
                    cc, slot = pend.pop(0)
                    k = st_["tm"] % 4
                    st_["tm"] += 1
                    hr, tmp = Tl.hr[slot], Tl.tmp[k]
                    S.op("dve", lambda e, o=tmp[:], i=Tl.Ysb[:, cc, tb * 512:(tb + 1) * 512], g=gpost[:, cc:cc + 1], r=Tl.rstd[tb][:]:
                         e.scalar_tensor_tensor(o, i, g, r, ALU.mult, ALU.mult),
                         reads=[("Y", cc, tb), ("rstd", tb)], writes=[("tmp", k)])
                    S.op("dve", lambda e, o=hr[:], a=hr[:], b=tmp[:]: e.tensor_tensor(o, a, b, ALU.add),
                         reads=[("tmp", k), ("hr", slot)], writes=[("hr", slot)])
                    S.dma("sp", lambda e, o=h_out[cc, :, ts:ts + 512], i=hr[:]: e.dma_start(out=o, in_=i),
                          reads=[("hr", slot)], writes=[("dram", "hout", cc, ts)])
                    yield

    def drain(g, n=None):
        if g is None:
            return None
        try:
            if n is None:
                while True:
                    next(g)
            else:
                for _ in range(n):
                    next(g)
        except StopIteration:
            return None
        return g

    npass = ntok // 1024
    drain(prenorm_gen(0))
    postg = None
    for p in range(npass):
        t0 = p * 1024
        it = 0
        for f in range(FC):
            wb = Tl.wb[f % 2]
            S.dma("poolq", lambda e, o=wb[:, 0:4096], i=wgu_t[f]: e.dma_start(out=o, in_=i),
                  reads=[], writes=[("wb", f % 2)])
            for tb in range(2):
                pg, pu = psG[it % 2], psU[it % 2]
                for kc in range(16):
                    _mm(S, pg[:], wb[:, kc * 128:(kc + 1) * 128], Tl.xnT[:, kc, tb * 512:(tb + 1) * 512], kc == 0, kc == 15,
                        reads=[("wb", f % 2), ("xn", kc, tb)], writes=[("ps", 2 * (it % 2))])
                for kc in range(16):
                    _mm(S, pu[:], wb[:, 2048 + kc * 128:2048 + (kc + 1) * 128], Tl.xnT[:, kc, tb * 512:(tb + 1) * 512],
                        kc == 0, kc == 15, reads=[("wb", f % 2), ("xn", kc, tb)], writes=[("ps", 2 * (it % 2) + 1)])
                sg = Tl.sg[it % 2]
                _act(S, sg[:], pg[:], AF.Silu, reads=[("ps", 2 * (it % 2))], writes=[("sg", it % 2)])
                S.op("dve", lambda e, o=Tl.AT[:, f, tb * 512:(tb + 1) * 512], a=sg[:], b=pu[:]: e.tensor_tensor(o, a, b, ALU.mult),
                     reads=[("sg", it % 2), ("ps", 2 * (it % 2) + 1)], writes=[("AT", f, tb)])
                it += 1
            if f >= 2:
                postg = drain(postg, 1)
        postg = drain(postg)
        preg = prenorm_gen(t0 + 1024) if p + 1 < npass else None
        it = 0
        for c in range(16):
            wb = Tl.wb[c % 2]
            S.dma("poolq", lambda e, o=wb[:, :], i=wd_t[c]: e.dma_start(out=o, in_=i), reads=[], writes=[("wb", c % 2)])
            for tb in range(2):
                py = psY[it % 2]
                for kc in range(FC):
                    _mm(S, py[:], wb[:, kc * 128:(kc + 1) * 128], Tl.AT[:, kc, tb * 512:(tb + 1) * 512], kc == 0, kc == FC - 1,
                        reads=[("wb", c % 2), ("AT", kc, tb)], writes=[("ps", 4 + it % 2)])
                k = st_["sq"] % 2
                st_["sq"] += 1
                sq = Tl.sq[k]
                _act(S, sq[:], py[:], AF.Square, reads=[("ps", 4 + it % 2)], writes=[("sq", k)])
                S.op("dve", lambda e, o=Tl.Ysb[:, c, tb * 512:(tb + 1) * 512], i=py[:]: e.tensor_copy(o, i),
                     reads=[("ps", 4 + it % 2)], writes=[("Y", c, tb)])
                _mm(S, psS[tb][:], ones[:], sq[:], c == 0, c == 15, reads=[("sq", k)], writes=[("ps", 6 + tb)])
                it += 1
                if c >= 1:
                    preg = drain(preg, 3)
        preg = drain(preg)
        postg = post_gen(t0)
    drain(postg)


class NormTiles:
    def __init__(self, nc, stack, pfx, ps_pre, ps_post, with_hs=True, with_hb=True):
        if with_hs:
            self.hs = stack.enter_context(_sb(nc, pfx + "_hs", [128, 16, 512], F32))
        self.sq = [stack.enter_context(_sb(nc, pfx + "_sq%d" % i, [128, 512], BF16)) for i in range(4)]
        self.rstd_pre = stack.enter_context(_sb(nc, pfx + "_rstdp", [128, 512], F32))
        self.rstd_post = [stack.enter_context(_sb(nc, pfx + "_rstdq%d" % i, [128, 512], F32)) for i in range(2)]
        self.with_hb = with_hb
        if with_hb:
            self.hb = stack.enter_context(_sb(nc, pfx + "_hb", [128, 16, 512], F32))
        self.tmp = [stack.enter_context(_sb(nc, pfx + "_tmp%d" % i, [128, 512], F32)) for i in range(2)]
        self.ps_pre = ps_pre
        self.ps_post = ps_post
        self.sqc = 0
        self.sqd = 0
        self.itc = 0


def prenorm512(S, N, C, h_in, ts, gain, out_fn):
    ones = C["ones_bf"]
    for c4 in range(4):
        src = h_in[c4 * 4:(c4 + 1) * 4, :, ts:ts + 512].rearrange("c p t -> p c t")
        S.dma("sp", lambda e, o=N.hs[:, c4 * 4:(c4 + 1) * 4, :], i=src: e.dma_start(out=o, in_=i),
              reads=[("dram", "h")], writes=[("hs", c4 * 4 + j) for j in range(4)])
    for c in range(16):
        k = N.sqc % 2
        _act(S, N.sq[k][:], N.hs[:, c, :], AF.Square, reads=[("hs", c)], writes=[("nsq", k)])
        _mm(S, N.ps_pre[:], ones[:], N.sq[k][:], c == 0, c == 15, reads=[("nsq", k)], writes=[("ps", "ssP")])
        N.sqc += 1
    _rstd_from_ss(S, N.rstd_pre[:], N.ps_pre[:], float(D), 1.0, reads=[("ps", "ssP")], writes=[("nrstdP",)])
    for c in range(16):
        o, wr = out_fn(c)
        S.op("dve", lambda e, o=o, i=N.hs[:, c, :], g=gain[:, c:c + 1], r=N.rstd_pre[:]:
             e.scalar_tensor_tensor(o, i, g, r, ALU.mult, ALU.mult),
             reads=[("hs", c), ("nrstdP",)], writes=wr)


def y_stats(S, N, C, ysb_c, yres, c, nchunks):
    k = 2 + N.sqd % 2
    _act(S, N.sq[k][:], ysb_c, AF.Square, reads=yres, writes=[("nsq", k)])
    _mm(S, N.ps_post[:], C["ones_bf"][:], N.sq[k][:], c == 0, c == nchunks - 1, reads=[("nsq", k)], writes=[("ps", "ssM")])
    N.sqd += 1


def post_rstd(S, N, par, nfeat, mult):
    _rstd_from_ss(S, N.rstd_post[par][:], N.ps_post[:], nfeat, mult, reads=[("ps", "ssM")], writes=[("nrstdQ", par)])


def postnorm_residual512(S, N, par, ysb, yres_fn, gain, h_in, h_out, ts):
    for c4 in range(4):
        src = h_in[c4 * 4:(c4 + 1) * 4, :, ts:ts + 512].rearrange("c p t -> p c t")
        S.dma("sp", lambda e, o=N.hb[:, c4 * 4:(c4 + 1) * 4, :], i=src: e.dma_start(out=o, in_=i),
              reads=[("dram", "h")], writes=[("nhb", c4 * 4 + j) for j in range(4)])
    for c in range(16):
        k = N.itc % 2
        tmp = N.tmp[k]
        S.op("dve", lambda e, o=tmp[:], i=ysb[:, c, :], g=gain[:, c:c + 1], r=N.rstd_post[par][:]:
             e.scalar_tensor_tensor(o, i, g, r, ALU.mult, ALU.mult),
             reads=list(yres_fn(c)) + [("nrstdQ", par)], writes=[("ntmp", k)])
        S.op("dve", lambda e, o=N.hb[:, c, :], a=N.hb[:, c, :], b=tmp[:]: e.tensor_tensor(o, a, b, ALU.add),
             reads=[("ntmp", k), ("nhb", c)], writes=[("nhb", c)])
        N.itc += 1
        if c % 4 == 3:
            c4 = c // 4
            dst = h_out[c4 * 4:(c4 + 1) * 4, :, ts:ts + 512].rearrange("c p t -> p c t")
            S.dma("sp", lambda e, o=dst, i=N.hb[:, c4 * 4:(c4 + 1) * 4, :]: e.dma_start(out=o, in_=i),
                  reads=[("nhb", c4 * 4 + j) for j in range(4)], writes=[("dram", "hout", c4, ts)])


def skew(n, stages):
    ns = len(stages)
    for t in range(n + ns - 1):
        for s, fn in enumerate(stages):
            i = t - s
            if 0 <= i < n:
                fn(i)


A_DIL = (1, 4, 16)


def mixA_qkv(nc, S, C, hT, ntok, wqkv_t, gain, qkvT):
    with ExitStack() as st:
        ps = [st.enter_context(_pst(nc, "qa_ps%d" % i, [128, 512], F32)) for i in range(3)]
        N = NormTiles(nc, st, "qa", ps[2], ps[2], with_hb=False)
        hnT = st.enter_context(_sb(nc, "qa_hnT", [128, 16, 2048], BF16))
        wq = [st.enter_context(_sb(nc, "qa_w%d" % i, [128, 2048], BF16)) for i in range(2)]
        stg = [st.enter_context(_sb(nc, "qa_stg%d" % i, [128, 2048], BF16)) for i in range(2)]
        nhalf = ntok // 2048
        for hf in range(nhalf):
            for tb in range(4):
                prenorm512(S, N, C, hT, hf * 2048 + tb * 512, gain,
                           lambda c, tb=tb: (hnT[:, c, tb * 512:(tb + 1) * 512], [("hnT", c, tb)]))
            it = 0
            for ch in range(72):
                s_, g_ = ch // 24, (ch // 8) % 3
                d = A_DIL[g_]
                L = ntok // d
                Lh = 2048 // d
                w = wq[ch % 2]
                S.dma("poolq", lambda e, o=w[:, :], i=wqkv_t[ch]: e.dma_start(out=o, in_=i), writes=[("qw", ch % 2)])
                sg = stg[ch % 2]
                sgv = sg[:, :].rearrange("p (r j) -> p r j", r=d)
                for tb in range(4):
                    p = ps[it % 2]
                    for kc in range(16):
                        _mm(S, p[:], w[:, kc * 128:(kc + 1) * 128], hnT[:, kc, tb * 512:(tb + 1) * 512], kc == 0, kc == 15,
                            reads=[("qw", ch % 2), ("hnT", kc, tb)], writes=[("ps", it % 2)])
                    jw = 512 // d
                    dst = sgv[:, :, tb * jw:(tb + 1) * jw]
                    src = p[:, :].rearrange("p (j r) -> p r j", r=d)
                    sc = 128.0 ** -0.5 if s_ == 0 else 1.0
                    if it % 2 == 0:
                        _act(S, dst, src, AF.Copy, reads=[("ps", it % 2)], writes=[("qstg", ch % 2)], scale=sc)
                    else:
                        S.op("dve", lambda e, o=dst, i=src, sc=sc: e.tensor_scalar(o, i, sc, None, ALU.mult),
                             reads=[("ps", it % 2)], writes=[("qstg", ch % 2)])
                    it += 1
                dd = qkvT[ch].rearrange("p (r j) -> p r j", r=d)[:, :, hf * Lh:(hf + 1) * Lh]
                S.dma("sp", lambda e, o=dd, i=sgv: e.dma_start(out=o, in_=i),
                      reads=[("qstg", ch % 2)], writes=[("dram", "qkvT", ch, hf)])
        S.flush()


def mixA_attn(nc, S, C, ntok, qkvT, alibi, pmask, Og):
    NU = ntok // 128
    with ExitStack() as st:
        psS = [st.enter_context(_pst(nc, "ab_psS%d" % i, [128, 2, 256], F32)) for i in range(2)]
        psT = [st.enter_context(_pst(nc, "ab_psT%d" % i, [128, 8, 128], BF16)) for i in range(2)]
        psO = [st.enter_context(_pst(nc, "ab_psO%d" % i, [128, 4, 128], F32)) for i in range(2)]
        psV = [st.enter_context(_pst(nc, "ab_psV%d" % i, [128, 8, 128], BF16)) for i in range(2)]
        qT = [st.enter_context(_sb(nc, "ab_q%d" % i, [128, ntok], BF16)) for i in range(2)]
        kT = [st.enter_context(_sb(nc, "ab_k%d" % i, [128, 128 + ntok], BF16)) for i in range(2)]
        vT = [st.enter_context(_sb(nc, "ab_v%d" % i, [128, ntok], BF16)) for i in range(2)]
        vtok = [st.enter_context(_sb(nc, "ab_vt%d" % i, [128, NU, 128], BF16)) for i in range(2)]
        bN = [st.enter_context(_sb(nc, "ab_bN%d" % i, [128, 256], F32)) for i in range(2)]
        bF = [st.enter_context(_sb(nc, "ab_bF%d" % i, [128, 256], F32)) for i in range(2)]
        Ssb = [st.enter_context(_sb(nc, "ab_S%d" % i, [128, 2, 256], F32)) for i in range(2)]
        Pb = [st.enter_context(_sb(nc, "ab_P%d" % i, [128, 2, 256], BF16)) for i in range(2)]
        PT = [st.enter_context(_sb(nc, "ab_PT%d" % i, [128, 4, 128], BF16)) for i in range(2)]
        osb = [st.enter_context(_sb(nc, "ab_o%d" % i, [128, NU, 128], F32)) for i in range(2)]
        lse = [st.enter_context(_sb(nc, "ab_lse%d" % i, [128, NU, 8], F32)) for i in range(2)]
        stt = [st.enter_context(_sb(nc, "ab_st%d" % i, [128, 16], F32)) for i in range(4)]
        ident = C["ident_bf"]
        for i in range(2):
            S.op("dve", lambda e, o=kT[i][:, 0:128]: e.memset(o, 0.0), writes=[("kT", i)])
        gh = 0
        bt = 0
        for g_ in range(3):
            d = A_DIL[g_]
            L = ntok // d
            nb = L // 128
            bO = (L // 2) // 128 if ntok == 4096 else -1
            for h_ in range(8):
                k2 = gh % 2
                chq, chk, chv = g_ * 8 + h_, 24 + g_ * 8 + h_, 48 + g_ * 8 + h_
                def loads(gh_):
                    gg, hh = gh_ // 8, gh_ % 8
                    kk2 = gh_ % 2
                    S.dma("sp", lambda e, o=qT[kk2][:, :], i=qkvT[gg * 8 + hh]: e.dma_start(out=o, in_=i),
                          reads=[("dram", "qkvT")], writes=[("qT", kk2)])
                    S.dma("sp", lambda e, o=kT[kk2][:, 128:], i=qkvT[24 + gg * 8 + hh]: e.dma_start(out=o, in_=i),
                          reads=[("dram", "qkvT")], writes=[("kT", kk2)])
                    S.dma("sp", lambda e, o=vT[kk2][:, :], i=qkvT[48 + gg * 8 + hh]: e.dma_start(out=o, in_=i),
                          reads=[("dram", "qkvT")], writes=[("vT", kk2)])
                    S.dma("sp", lambda e, o=bN[kk2][:, :], i=alibi[gg * 8 + hh]: e.dma_start(out=o, in_=i), writes=[("bN", kk2)])
                    S.op("dve", lambda e, o=bF[kk2][:, 128:256], i=bN[kk2][:, 128:256]: e.tensor_copy(o, i),
                         reads=[("bN", kk2)], writes=[("bF", kk2)])
                    S.op("dve", lambda e, o=bF[kk2][:, 0:128]: e.memset(o, NEGB), writes=[("bF", kk2)])
                if gh == 0:
                    loads(0)
                if gh + 1 < 24:
                    loads(gh + 1)
                for q4 in range(NU // 4):
                    pv = psV[q4 % 2]
                    for j in range(4):
                        blk = q4 * 4 + j
                        S.op("pe", lambda e, o=pv[:, j, :], i=vT[k2][:, blk * 128:(blk + 1) * 128]: e.transpose(o, i, ident[:]),
                             reads=[("vT", k2)], writes=[("ps", "V", q4 % 2)])
                    if q4 % 2 == 0:
                        _act(S, vtok[k2][:, q4 * 4:(q4 + 1) * 4, :], pv[:, 0:4, :], AF.Copy,
                             reads=[("ps", "V", q4 % 2)], writes=[("vtok", k2)])
                    else:
                        S.op("dve", lambda e, o=vtok[k2][:, q4 * 4:(q4 + 1) * 4, :], i=pv[:, 0:4, :]: e.tensor_copy(o, i),
                             reads=[("ps", "V", q4 % 2)], writes=[("vtok", k2)])
                nbat = NU // 2

                def stA(bi, k2=k2, nb=nb, bO=bO):
                    u0 = bi * 2
                    k = (bt0 + bi) % 2
                    pS, sS = psS[k], Ssb[k]
                    for j in range(2):
                        u = u0 + j
                        S.op("pe", lambda e, o=pS[:, j, :], l=qT[k2][:, u * 128:(u + 1) * 128], r=kT[k2][:, u * 128:u * 128 + 256]:
                             e.matmul(o, l, r, start=True, stop=True),
                             reads=[("qT", k2), ("kT", k2)], writes=[("ps", "S", k)])
                    for j in range(2):
                        u = u0 + j
                        b = u % nb
                        bias = bF[k2] if b == 0 else bN[k2]
                        S.op("dve", lambda e, o=sS[:, j, :], a=pS[:, j, :], bb=bias[:, :]: e.tensor_tensor(o, a, bb, ALU.add),
                             reads=[("ps", "S", k), ("bN", k2), ("bF", k2)], writes=[("Ssb", k)])
                        if b == bO:
                            S.op("dve", lambda e, o=sS[:, j, 0:128]: e.tensor_scalar(o, o, pmask[:, 0:1], None, ALU.add),
                                 reads=[("Ssb", k)], writes=[("Ssb", k)])

                def stB(bi):
                    k = (bt0 + bi) % 2
                    k4 = (bt0 + bi) % 4
                    sS, pb, sv = Ssb[k], Pb[k], stt[k4]
                    S.op("dve", lambda e, o=sv[:, 2:4], i=sS[:, :, :]: e.tensor_reduce(o, i, AX.X, ALU.max, negate=True),
                         reads=[("Ssb", k)], writes=[("sv", k4)])
                    for j in range(2):
                        _act(S, pb[:, j, :], sS[:, j, :], AF.Exp, reads=[("Ssb", k), ("sv", k4)], writes=[("Pb", k), ("sv", k4)],
                             bias=sv[:, 2 + j:3 + j], accum_out=sv[:, 4 + j:5 + j])

                def stC(bi):
                    k = (bt0 + bi) % 2
                    pb, pT_, pt = Pb[k], psT[k], PT[k]
                    for j in range(2):
                        for kk in range(2):
                            S.op("pe", lambda e, o=pT_[:, j * 2 + kk, :], i=pb[:, j, kk * 128:(kk + 1) * 128]: e.transpose(o, i, ident[:]),
                                 reads=[("Pb", k)], writes=[("ps", "T", k)])
                    _act(S, pt[:, :, :], pT_[:, 0:4, :], AF.Copy, reads=[("ps", "T", k)], writes=[("PT", k)])

                def stD(bi, k2=k2, g_=g_, h_=h_):
                    u0 = bi * 2
                    k = (bt0 + bi) % 2
                    k4 = (bt0 + bi) % 4
                    pt, pO, sv = PT[k], psO[k], stt[k4]
                    for j in range(2):
                        u = u0 + j
                        for kk in range(2):
                            vb = max(u - 1 + kk, 0)
                            S.op("pe", lambda e, o=pO[:, j, :], l=pt[:, j * 2 + kk, :], r=vtok[k2][:, vb, :], st_=(kk == 0), sp_=(kk == 1):
                                 e.matmul(o, l, r, start=st_, stop=sp_),
                                 reads=[("PT", k), ("vtok", k2)], writes=[("ps", "O", k)])
                    S.op("dve", lambda e, o=sv[:, 6:8], i=sv[:, 4:6]: e.reciprocal(o, i), reads=[("sv", k4)], writes=[("sv", k4)])
                    for j in range(2):
                        S.op("dve", lambda e, o=osb[k2][:, u0 + j, :], i=pO[:, j, :], r=sv[:, 6 + j:7 + j]:
                             e.tensor_scalar(o, i, r, None, ALU.mult),
                             reads=[("ps", "O", k), ("sv", k4)], writes=[("osb", k2)])
                    _act(S, sv[:, 8:10], sv[:, 4:6], AF.Ln, reads=[("sv", k4)], writes=[("sv", k4)])
                    S.op("pool", lambda e, o=lse[g_ % 2][:, u0:u0 + 2, h_], a=sv[:, 8:10], bb=sv[:, 2:4]: e.tensor_tensor(o, a, bb, ALU.subtract),
                         reads=[("sv", k4)], writes=[("lse", g_ % 2)])

                bt0 = bt
                for t in range(nbat + 3):
                    if t < nbat:
                        stA(t)
                    if 0 <= t - 1 < nbat:
                        stB(t - 1)
                    if 0 <= t - 2 < nbat:
                        stC(t - 2)
                    if 0 <= t - 3 < nbat:
                        stD(t - 3)
                bt += nbat
                for r in range(d):
                    dst = Og[g_][:, h_ * 128:(h_ + 1) * 128].rearrange("(b i r) c -> i r b c", i=128, r=d)[:, r, :, :]
                    srcv = osb[k2][:, r * nb:(r + 1) * nb, :]
                    S.dma("sp", lambda e, o=dst, i=srcv: e.dma_start(out=o, in_=i),
                          reads=[("osb", k2)], writes=[("dram", "Og", g_, h_, r)])
                gh += 1
            for r in range(d):
                dst = Og[g_][:, 1024:1032].rearrange("(b i r) c -> i r b c", i=128, r=d)[:, r, :, :]
                srcv = lse[g_ % 2][:, r * nb:(r + 1) * nb, :]
                S.dma("sp", lambda e, o=dst, i=srcv: e.dma_start(out=o, in_=i),
                      reads=[("lse", g_ % 2)], writes=[("dram", "OgL", g_, r)])
        S.flush()


def mixA_out(nc, S, C, hT, ntok, Og, wo_t, gain):
    with ExitStack() as st:
        psT = [st.enter_context(_pst(nc, "ac_psT%d" % i, [128, 8, 128], BF16)) for i in range(2)]
        psM = [st.enter_context(_pst(nc, "ac_psM%d" % i, [128, 512], F32)) for i in range(2)]
        psS = st.enter_context(_pst(nc, "ac_psS", [128, 512], F32))
        N = NormTiles(nc, st, "ac", psS, psS, with_hs=False)
        og = [st.enter_context(_sb(nc, "ac_og%d" % i, [128, 3, 1032], F32)) for i in range(2)]
        wt = [st.enter_context(_sb(nc, "ac_w%d" % i, [128, 64], F32)) for i in range(2)]
        acc = [st.enter_context(_sb(nc, "ac_acc%d" % i, [128, 1024], F32)) for i in range(2)]
        otok = [st.enter_context(_sb(nc, "ac_ot%d" % i, [128, 1024], BF16)) for i in range(2)]
        oT = [st.enter_context(_sb(nc, "ac_oT%d" % i, [128, 8, 512], BF16)) for i in range(2)]
        wo = st.enter_context(_sb(nc, "ac_wo", [128, 16 * 8 * 128], BF16))
        Ysb = [st.enter_context(_sb(nc, "ac_Y%d" % i, [128, 16, 512], F32)) for i in range(2)]
        ident = C["ident_bf"]
        for q in range(4):
            S.dma("poolq", lambda e, o=wo[:, q * 4096:(q + 1) * 4096], i=wo_t[:, q * 4096:(q + 1) * 4096]: e.dma_start(out=o, in_=i),
                  writes=[("wo", q)])

        def stG(tb):
            oTt = oT[tb % 2]
            for tt in range(4):
                ti = tb * 4 + tt
                k = ti % 2
                t0 = tb * 512 + tt * 128
                for g_ in range(3):
                    S.dma("sp", lambda e, o=og[k][:, g_, :], i=Og[g_][t0:t0 + 128, :]: e.dma_start(out=o, in_=i),
                          reads=[("dram", "Og")], writes=[("og", k, g_)])
                w = wt[k]
                ogr = [("og", k, g_) for g_ in range(3)]
                S.op("dve", lambda e, o=w[:, 0:8], a=og[k][:, 0, 1024:1032], b=og[k][:, 1, 1024:1032]: e.tensor_tensor(o, a, b, ALU.max),
                     reads=ogr, writes=[("wt", k)])
                S.op("dve", lambda e, o=w[:, 0:8], a=w[:, 0:8], b=og[k][:, 2, 1024:1032]: e.tensor_tensor(o, a, b, ALU.max),
                     reads=ogr + [("wt", k)], writes=[("wt", k)])
                for g_ in range(3):
                    S.op("dve", lambda e, o=w[:, 8 + g_ * 8:16 + g_ * 8], a=og[k][:, g_, 1024:1032], b=w[:, 0:8]:
                         e.tensor_tensor(o, a, b, ALU.subtract), reads=ogr + [("wt", k)], writes=[("wt", k)])
                _act(S, w[:, 8:32], w[:, 8:32], AF.Exp, reads=[("wt", k)], writes=[("wt", k)])
                S.op("dve", lambda e, o=w[:, 32:40], a=w[:, 8:16], b=w[:, 16:24]: e.tensor_tensor(o, a, b, ALU.add),
                     reads=[("wt", k)], writes=[("wt", k)])
                S.op("dve", lambda e, o=w[:, 32:40], a=w[:, 32:40], b=w[:, 24:32]: e.tensor_tensor(o, a, b, ALU.add),
                     reads=[("wt", k)], writes=[("wt", k)])
                S.op("dve", lambda e, o=w[:, 40:48], i=w[:, 32:40]: e.reciprocal(o, i), reads=[("wt", k)], writes=[("wt", k)])
                for g_ in range(3):
                    S.op("dve", lambda e, o=w[:, 8 + g_ * 8:16 + g_ * 8], a=w[:, 8 + g_ * 8:16 + g_ * 8], b=w[:, 40:48]:
                         e.tensor_tensor(o, a, b, ALU.mult), reads=[("wt", k)], writes=[("wt", k)])
                for h_ in range(8):
                    hs_ = slice(h_ * 128, (h_ + 1) * 128)
                    e0 = "dve"
                    S.op(e0, lambda e, o=acc[k][:, hs_], i=og[k][:, 0, hs_], s=w[:, 8 + h_:9 + h_]: e.tensor_scalar(o, i, s, None, ALU.mult),
                         reads=ogr + [("wt", k)], writes=[("acc", k, h_)])
                    S.op(e0, lambda e, o=acc[k][:, hs_], i=og[k][:, 1, hs_], s=w[:, 16 + h_:17 + h_], a=acc[k][:, hs_]:
                         e.scalar_tensor_tensor(o, i, s, a, ALU.mult, ALU.add),
                         reads=ogr + [("wt", k), ("acc", k, h_)], writes=[("acc", k, h_)])
                    S.op(e0, lambda e, o=otok[k][:, hs_], i=og[k][:, 2, hs_], s=w[:, 24 + h_:25 + h_], a=acc[k][:, hs_]:
                         e.scalar_tensor_tensor(o, i, s, a, ALU.mult, ALU.add),
                         reads=ogr + [("wt", k), ("acc", k, h_)], writes=[("otok", k, h_)])
                pT_ = psT[k]
                for h_ in range(8):
                    S.op("pe", lambda e, o=pT_[:, h_, :], i=otok[k][:, h_ * 128:(h_ + 1) * 128]: e.transpose(o, i, ident[:]),
                         reads=[("otok", k, h_)], writes=[("ps", "T", k)])
                _act(S, oTt[:, :, tt * 128:(tt + 1) * 128], pT_[:, :, :], AF.Copy, reads=[("ps", "T", k)], writes=[("oT", tb % 2, tt)])

        def stM(tb):
            oTt = oT[tb % 2]
            Y = Ysb[tb % 2]
            for c in range(16):
                pm = psM[c % 2]
                for kc in range(8):
                    _mm(S, pm[:], wo[:, (c * 8 + kc) * 128:(c * 8 + kc + 1) * 128], oTt[:, kc, :], kc == 0, kc == 7,
                        reads=[("wo", c // 4)] + [("oT", tb % 2, tt) for tt in range(4)], writes=[("ps", "M", c % 2)])
                S.op("dve", lambda e, o=Y[:, c, :], i=pm[:]: e.tensor_copy(o, i), reads=[("ps", "M", c % 2)], writes=[("Y", tb % 2, c)])
                y_stats(S, N, C, Y[:, c, :], [("Y", tb % 2, c)], c, 16)
            post_rstd(S, N, tb % 2, float(D), 1.0)

        def stR(tb):
            postnorm_residual512(S, N, tb % 2, Ysb[tb % 2], lambda c: [("Y", tb % 2, c)], gain, hT, hT, tb * 512)

        skew(ntok // 512, [stG, stM, stR])
        S.flush()


TWO_PI = 6.283185307179586
PI = 3.141592653589793


def rope_tables(nc, S, C, pos_d, ntok, ropec, cosT, sinT):
    with ExitStack() as st:
        pi_ = st.enter_context(_sb(nc, "rp_pi", [64, 2048], I32))
        ang = st.enter_context(_sb(nc, "rp_ang", [64, 2048], F32))
        u = st.enter_context(_sb(nc, "rp_u", [64, 2048], F32))
        ki = st.enter_context(_sb(nc, "rp_ki", [64, 2048], I32))
        kf = st.enter_context(_sb(nc, "rp_kf", [64, 2048], F32))
        res = st.enter_context(_sb(nc, "rp_res", [64, 2048], F32))
        for hf in range(ntok // 2048):
            sl = slice(hf * 2048, (hf + 1) * 2048)
            S.dma("sp", lambda e, i=pos_d[0:1, sl].partition_broadcast(64): e.dma_start(out=pi_[:, :], in_=i), writes=[("rp_pi",)])
            S.op("dve", lambda e: e.tensor_copy(ang[:, :], pi_[:, :]), reads=[("rp_pi",)], writes=[("rp_ang",)])
            S.op("dve", lambda e: e.tensor_scalar(ang[:, :], ang[:, :], ropec[:, 0:1], None, ALU.mult),
                 reads=[("rp_ang",)], writes=[("rp_ang",)])
            for which, shift, dst in (("sin", 0.0, sinT), ("cos", PI / 2, cosT)):
                S.op("dve", lambda e, sh=shift: e.tensor_scalar(u[:, :], ang[:, :], sh, None, ALU.add),
                     reads=[("rp_ang",)], writes=[("rp_u",)])
                S.op("dve", lambda e: e.tensor_scalar(kf[:, :], u[:, :], 1.0 / TWO_PI, None, ALU.mult),
                     reads=[("rp_u",)], writes=[("rp_kf",)])
                S.op("dve", lambda e: e.tensor_copy(ki[:, :], kf[:, :]), reads=[("rp_kf",)], writes=[("rp_ki",)])
                S.op("dve", lambda e: e.tensor_copy(kf[:, :], ki[:, :]), reads=[("rp_ki",)], writes=[("rp_kf",)])
                S.op("dve", lambda e: e.scalar_tensor_tensor(u[:, :], kf[:, :], -TWO_PI, u[:, :], ALU.mult, ALU.add),
                     reads=[("rp_kf",), ("rp_u",)], writes=[("rp_u",)])
                S.op("dve", lambda e: e.tensor_scalar(kf[:, :], u[:, :], PI, -TWO_PI, ALU.is_gt, ALU.mult),
                     reads=[("rp_u",)], writes=[("rp_kf",)])
                S.op("dve", lambda e: e.tensor_tensor(u[:, :], u[:, :], kf[:, :], ALU.add),
                     reads=[("rp_kf",), ("rp_u",)], writes=[("rp_u",)])
                S.op("dve", lambda e: e.tensor_scalar(kf[:, :], u[:, :], -PI, TWO_PI, ALU.is_lt, ALU.mult),
                     reads=[("rp_u",)], writes=[("rp_kf",)])
                S.op("dve", lambda e: e.tensor_tensor(u[:, :], u[:, :], kf[:, :], ALU.add),
                     reads=[("rp_kf",), ("rp_u",)], writes=[("rp_u",)])
                S.op("dve", lambda e: e.tensor_scalar(u[:, :], u[:, :], PI, -PI, ALU.min, ALU.max),
                     reads=[("rp_u",)], writes=[("rp_u",)])
                _act(S, res[:, :], u[:, :], AF.Sin, reads=[("rp_u",)], writes=[("rp_res",)])
                if which == "sin":
                    S.op("dve", lambda e: e.tensor_scalar(res[:, :], res[:, :], ropec[:, 1:2], None, ALU.mult),
                         reads=[("rp_res",)], writes=[("rp_res",)])
                S.dma("sp", lambda e, o=dst[:, sl]: e.dma_start(out=o, in_=res[:, :]), reads=[("rp_res",)], writes=[("dram", which, hf)])
        S.flush()


def ple_block(nc, S, C, h_in, h_out, ntok, pT, wg_t, wp_t, gpre, gpost):
    with ExitStack() as st:
        ps = [st.enter_context(_pst(nc, "pl_ps%d" % i, [128, 512], F32)) for i in range(4)]
        psP = st.enter_context(_pst(nc, "pl_psP", [128, 512], F32))
        psQ = st.enter_context(_pst(nc, "pl_psQ", [128, 512], F32))
        N = NormTiles(nc, st, "pl", psP, psQ)
        xn = [st.enter_context(_sb(nc, "pl_xn%d" % i, [128, 16, 512], BF16)) for i in range(2)]
        pb = [st.enter_context(_sb(nc, "pl_pb%d" % i, [128, 2, 512], BF16)) for i in range(2)]
        wg = [st.enter_context(_sb(nc, "pl_wg%d" % i, [128, 2048], BF16)) for i in range(2)]
        wp = st.enter_context(_sb(nc, "pl_wp", [128, 2, 2048], BF16))
        gt = [st.enter_context(_sb(nc, "pl_gt%d" % i, [128, 512], F32)) for i in range(2)]
        Ysb = [st.enter_context(_sb(nc, "pl_Y%d" % i, [128, 16, 512], F32)) for i in range(2)]
        S.dma("poolq", lambda e: e.dma_start(out=wp[:, :, :], in_=wp_t.rearrange("p (k n) -> p k n", k=2)), writes=[("wp",)])

        def stP(tb):
            ts = tb * 512
            prenorm512(S, N, C, h_in, ts, gpre, lambda c: (xn[tb % 2][:, c, :], [("xn", tb % 2, c)]))
            S.dma("poolq", lambda e, o=pb[tb % 2][:, :, :], i=pT[:, :, ts:ts + 512].rearrange("k p t -> p k t"): e.dma_start(out=o, in_=i),
                  writes=[("pb", tb % 2)])

        def stM(tb):
            Y = Ysb[tb % 2]
            x = xn[tb % 2]
            for c in range(16):
                k = c % 2
                S.dma("poolq", lambda e, o=wg[k][:, :], i=wg_t[c]: e.dma_start(out=o, in_=i), writes=[("wg", k)])
                pg, pe_ = ps[2 * k], ps[2 * k + 1]
                for kc in range(16):
                    _mm(S, pg[:], wg[k][:, kc * 128:(kc + 1) * 128], x[:, kc, :], kc == 0, kc == 15,
                        reads=[("wg", k), ("xn", tb % 2, kc)], writes=[("ps", 2 * k)])
                for k2 in range(2):
                    _mm(S, pe_[:], wp[:, k2, c * 128:(c + 1) * 128], pb[tb % 2][:, k2, :], k2 == 0, k2 == 1,
                        reads=[("wp",), ("pb", tb % 2)], writes=[("ps", 2 * k + 1)])
                _act(S, gt[k][:], pg[:], AF.Sigmoid, reads=[("ps", 2 * k)], writes=[("gt", k)])
                S.op("dve", lambda e, o=Y[:, c, :], a=gt[k][:], b=pe_[:]: e.tensor_tensor(o, a, b, ALU.mult),
                     reads=[("gt", k), ("ps", 2 * k + 1)], writes=[("Y", tb % 2, c)])
                y_stats(S, N, C, Y[:, c, :], [("Y", tb % 2, c)], c, 16)
            post_rstd(S, N, tb % 2, float(D), 1.0)

        def stR(tb):
            postnorm_residual512(S, N, tb % 2, Ysb[tb % 2], lambda c: [("Y", tb % 2, c)], gpost, h_in, h_out, tb * 512)

        skew(ntok // 512, [stP, stM, stR])
        S.flush()


def rope_combine(S, out_bf, ps_r, ps_rs, cos_sb, sin_sb, t1, t2, res_r, res_rs, wr, scale=1.0, csres=("cs",)):
    S.op("dve", lambda e: e.tensor_tensor(t1, ps_r, cos_sb, ALU.mult), reads=[res_r, csres], writes=[("rt1",)])
    S.op("dve", lambda e: e.tensor_tensor(t2, ps_rs, sin_sb, ALU.mult), reads=[res_rs, csres], writes=[("rt2",)])
    if scale == 1.0:
        S.op("dve", lambda e: e.tensor_tensor(out_bf, t1, t2, ALU.add), reads=[("rt1",), ("rt2",)], writes=wr)
    else:
        S.op("dve", lambda e: e.scalar_tensor_tensor(t1, t1, 1.0, t2, ALU.mult, ALU.add), reads=[("rt1",), ("rt2",)], writes=[("rt1",)])
        S.op("dve", lambda e: e.tensor_scalar(out_bf, t1, scale, None, ALU.mult), reads=[("rt1",)], writes=wr)


def shared_kv(nc, S, C, hT, ntok, wdkv_t, wukv_k_t, wukv_v_t, g_in, g_kv, cosT, sinT, KT, Vtok, krT):
    with ExitStack() as st:
        ps = [st.enter_context(_pst(nc, "kv_ps%d" % i, [128, 512], F32)) for i in range(4)]
        psR = [st.enter_context(_pst(nc, "kv_psR%d" % i, [128, 512], F32)) for i in range(2)]
        psS = st.enter_context(_pst(nc, "kv_psS", [128, 512], F32))
        psS2 = st.enter_context(_pst(nc, "kv_psS2", [128, 512], F32))
        N = NormTiles(nc, st, "kv", psS, psS2, with_hb=False)
        hk2 = [st.enter_context(_sb(nc, "kv_hk%d" % i, [128, 16, 512], BF16)) for i in range(2)]
        wd = st.enter_context(_sb(nc, "kv_wd", [128, 16, 640], BF16))
        wk = st.enter_context(_sb(nc, "kv_wk", [128, 4, 2048], BF16))
        wv = st.enter_context(_sb(nc, "kv_wv", [128, 4, 2048], BF16))
        ck = st.enter_context(_sb(nc, "kv_ck", [128, 4, 512], F32))
        ckn = st.enter_context(_sb(nc, "kv_ckn", [128, 4, 512], BF16))
        cs2 = [st.enter_context(_sb(nc, "kv_cs%d" % i, [64, 2, 512], F32)) for i in range(2)]
        t1 = st.enter_context(_sb(nc, "kv_t1", [64, 512], F32))
        t2 = st.enter_context(_sb(nc, "kv_t2", [64, 512], F32))
        krs = st.enter_context(_sb(nc, "kv_kr", [64, 512], BF16))
        kst = [st.enter_context(_sb(nc, "kv_kst%d" % i, [128, 512], BF16)) for i in range(2)]
        vst = [st.enter_context(_sb(nc, "kv_vst%d" % i, [128, 2048], BF16)) for i in range(2)]
        S.dma("poolq", lambda e: e.dma_start(out=wd[:, :, :], in_=wdkv_t.rearrange("p (k n) -> p k n", k=16)), writes=[("wd",)])
        S.dma("poolq", lambda e: e.dma_start(out=wk[:, :, :], in_=wukv_k_t.rearrange("p (k n) -> p k n", k=4)), writes=[("wk",)])
        S.dma("poolq", lambda e: e.dma_start(out=wv[:, :, :], in_=wukv_v_t.rearrange("p (k n) -> p k n", k=4)), writes=[("wv",)])
        cnt = {"it": 0, "vi": 0}

        def stP(tb):
            ts = tb * 512
            par = tb % 2
            prenorm512(S, N, C, hT, ts, g_in, lambda c: (hk2[par][:, c, :], [("hk", par, c)]))
            S.dma("sp", lambda e, i=cosT[:, ts:ts + 512]: e.dma_start(out=cs2[par][:, 0, :], in_=i), writes=[("cs", par)])
            S.dma("sp", lambda e, i=sinT[:, ts:ts + 512]: e.dma_start(out=cs2[par][:, 1, :], in_=i), writes=[("cs", par)])

        def stM(tb):
            ts = tb * 512
            par = tb % 2
            hk = hk2[par]
            cs = cs2[par]
            it = cnt["it"]
            vi = cnt["vi"]
            for j in range(4):
                p = ps[it % 4]
                for kc in range(16):
                    _mm(S, p[:], wd[:, kc, j * 128:(j + 1) * 128], hk[:, kc, :], kc == 0, kc == 15,
                        reads=[("wd",), ("hk", par, kc)], writes=[("ps", it % 4)])
                S.op("dve", lambda e, o=ck[:, j, :], i=p[:]: e.tensor_copy(o, i), reads=[("ps", it % 4)], writes=[("ck", j)])
                y_stats(S, N, C, ck[:, j, :], [("ck", j)], j, 4)
                it += 1
            for q, col in ((0, 512), (1, 576)):
                for kc in range(16):
                    _mm(S, psR[q][0:64, :], wd[:, kc, col:col + 64], hk[:, kc, :], kc == 0, kc == 15,
                        reads=[("wd",), ("hk", par, kc)], writes=[("ps", "R", q)])
            rope_combine(S, krs[:, :], psR[0][0:64, :], psR[1][0:64, :], cs[:, 0, :], cs[:, 1, :], t1[:, :], t2[:, :],
                         ("ps", "R", 0), ("ps", "R", 1), [("krs",)], csres=("cs", par))
            S.dma("sp", lambda e, o=krT[:, ts:ts + 512]: e.dma_start(out=o, in_=krs[:, :]), reads=[("krs",)], writes=[("dram", "krT", tb)])
            post_rstd(S, N, 0, 512.0, 1.0)
            for j in range(4):
                S.op("dve", lambda e, o=ckn[:, j, :], i=ck[:, j, :], g=g_kv[:, j:j + 1]:
                     e.scalar_tensor_tensor(o, i, g, N.rstd_post[0][:], ALU.mult, ALU.mult),
                     reads=[("ck", j), ("nrstdQ", 0)], writes=[("ckn", j)])
            for hd in range(16):
                p = ps[it % 4]
                for j in range(4):
                    _mm(S, p[:], wk[:, j, hd * 128:(hd + 1) * 128], ckn[:, j, :], j == 0, j == 3,
                        reads=[("wk",), ("ckn", j)], writes=[("ps", it % 4)])
                ks = kst[hd % 2]
                if hd % 2 == 0:
                    _act(S, ks[:, :], p[:], AF.Copy, reads=[("ps", it % 4)], writes=[("kst", hd % 2)])
                else:
                    S.op("dve", lambda e, o=ks[:, :], i=p[:]: e.tensor_copy(o, i), reads=[("ps", it % 4)], writes=[("kst", hd % 2)])
                S.dma("sp", lambda e, o=KT[hd][:, ts:ts + 512], i=ks[:, :]: e.dma_start(out=o, in_=i),
                      reads=[("kst", hd % 2)], writes=[("dram", "KT", hd, tb)])
                it += 1
            for tt in range(4):
                vs = vst[vi % 2]
                for vb in range(4):
                    p = ps[it % 4]
                    for j in range(4):
                        _mm(S, p[:], ckn[:, j, tt * 128:(tt + 1) * 128], wv[:, j, vb * 512:(vb + 1) * 512], j == 0, j == 3,
                            reads=[("wv",), ("ckn", j)], writes=[("ps", it % 4)])
                    if vb % 2 == 0:
                        _act(S, vs[:, vb * 512:(vb + 1) * 512], p[:], AF.Copy, reads=[("ps", it % 4)], writes=[("vst", vi % 2)])
                    else:
                        S.op("dve", lambda e, o=vs[:, vb * 512:(vb + 1) * 512], i=p[:]: e.tensor_copy(o, i),
                             reads=[("ps", it % 4)], writes=[("vst", vi % 2)])
                    it += 1
                S.dma("sp", lambda e, o=Vtok[tb * 4 + tt], i=vs[:, :]: e.dma_start(out=o, in_=i),
                      reads=[("vst", vi % 2)], writes=[("dram", "Vtok", tb, tt)])
                vi += 1
            cnt["it"] = it
            cnt["vi"] = vi

        skew(ntok // 512, [stP, stM])
        S.flush()


MLA_SCALE = 192.0 ** -0.5


def mla_q(nc, S, C, h_in, ntok, tok_off, wdq_t, wuqn_t, wuqr_t, wuqrs_t, gpre, g_q, cosT, sinT, qnT, qrT):
    with ExitStack() as st:
        ps = [st.enter_context(_pst(nc, "mq_ps%d" % i, [128, 512], F32)) for i in range(2)]
        psR = [st.enter_context(_pst(nc, "mq_psR%d" % i, [128, 512], F32)) for i in range(4)]
        psS = st.enter_context(_pst(nc, "mq_psS", [128, 512], F32))
        psS2 = st.enter_context(_pst(nc, "mq_psS2", [128, 512], F32))
        N = NormTiles(nc, st, "mq", psS, psS2, with_hb=False)
        hn2 = [st.enter_context(_sb(nc, "mq_hn%d" % i, [128, 16, 512], BF16)) for i in range(2)]
        wd = st.enter_context(_sb(nc, "mq_wd", [128, 16, 512], BF16))
        wn = st.enter_context(_sb(nc, "mq_wn", [128, 4, 2048], BF16))
        wr = st.enter_context(_sb(nc, "mq_wr", [128, 4, 1024], BF16))
        wrs = st.enter_context(_sb(nc, "mq_wrs", [128, 4, 1024], BF16))
        cq = st.enter_context(_sb(nc, "mq_cq", [128, 4, 512], F32))
        cqn = st.enter_context(_sb(nc, "mq_cqn", [128, 4, 512], BF16))
        cs2 = [st.enter_context(_sb(nc, "mq_cs%d" % i, [64, 2, 512], F32)) for i in range(2)]
        t1 = st.enter_context(_sb(nc, "mq_t1", [64, 512], F32))
        t2 = st.enter_context(_sb(nc, "mq_t2", [64, 512], F32))
        qst = [st.enter_context(_sb(nc, "mq_qst%d" % i, [128, 512], BF16)) for i in range(2)]
        rst = [st.enter_context(_sb(nc, "mq_rst%d" % i, [64, 512], BF16)) for i in range(2)]
        S.dma("poolq", lambda e: e.dma_start(out=wd[:, :, :], in_=wdq_t.rearrange("p (k n) -> p k n", k=16)), writes=[("wd",)])
        S.dma("poolq", lambda e: e.dma_start(out=wn[:, :, :], in_=wuqn_t.rearrange("p (k n) -> p k n", k=4)), writes=[("wn",)])
        S.dma("poolq", lambda e: e.dma_start(out=wr[:, :, :], in_=wuqr_t.rearrange("p (k n) -> p k n", k=4)), writes=[("wr",)])
        S.dma("poolq", lambda e: e.dma_start(out=wrs[:, :, :], in_=wuqrs_t.rearrange("p (k n) -> p k n", k=4)), writes=[("wrs",)])
        cnt = {"it": 0, "ir": 0}

        def stP(tb):
            ts = tb * 512
            par = tb % 2
            prenorm512(S, N, C, h_in, ts, gpre, lambda c: (hn2[par][:, c, :], [("hn", par, c)]))
            S.dma("sp", lambda e, i=cosT[:, tok_off + ts:tok_off + ts + 512]: e.dma_start(out=cs2[par][:, 0, :], in_=i), writes=[("cs", par)])
            S.dma("sp", lambda e, i=sinT[:, tok_off + ts:tok_off + ts + 512]: e.dma_start(out=cs2[par][:, 1, :], in_=i), writes=[("cs", par)])

        def stM(tb):
            ts = tb * 512
            par = tb % 2
            hn = hn2[par]
            cs = cs2[par]
            it = cnt["it"]
            ir = cnt["ir"]
            for j in range(4):
                p = ps[it % 2]
                for kc in range(16):
                    _mm(S, p[:], wd[:, kc, j * 128:(j + 1) * 128], hn[:, kc, :], kc == 0, kc == 15,
                        reads=[("wd",), ("hn", par, kc)], writes=[("ps", it % 2)])
                S.op("dve", lambda e, o=cq[:, j, :], i=p[:]: e.tensor_copy(o, i), reads=[("ps", it % 2)], writes=[("cq", j)])
                y_stats(S, N, C, cq[:, j, :], [("cq", j)], j, 4)
                it += 1
            post_rstd(S, N, 0, 512.0, 1.0)
            for j in range(4):
                S.op("dve", lambda e, o=cqn[:, j, :], i=cq[:, j, :], g=g_q[:, j:j + 1]:
                     e.scalar_tensor_tensor(o, i, g, N.rstd_post[0][:], ALU.mult, ALU.mult),
                     reads=[("cq", j), ("nrstdQ", 0)], writes=[("cqn", j)])
            for hd in range(16):
                p = ps[it % 2]
                for j in range(4):
                    _mm(S, p[:], wn[:, j, hd * 128:(hd + 1) * 128], cqn[:, j, :], j == 0, j == 3,
                        reads=[("wn",), ("cqn", j)], writes=[("ps", it % 2)])
                qs = qst[hd % 2]
                _act(S, qs[:, :], p[:], AF.Copy, reads=[("ps", it % 2)], writes=[("qst", hd % 2)], scale=MLA_SCALE)
                S.dma("sp", lambda e, o=qnT[hd][:, ts:ts + 512], i=qs[:, :]: e.dma_start(out=o, in_=i),
                      reads=[("qst", hd % 2)], writes=[("dram", "qnT", hd, tb)])
                it += 1
                pr, prs = psR[2 * (ir % 2)], psR[2 * (ir % 2) + 1]
                for j in range(4):
                    _mm(S, pr[0:64, :], wr[:, j, hd * 64:(hd + 1) * 64], cqn[:, j, :], j == 0, j == 3,
                        reads=[("wr",), ("cqn", j)], writes=[("ps", "R", 2 * (ir % 2))])
                for j in range(4):
                    _mm(S, prs[0:64, :], wrs[:, j, hd * 64:(hd + 1) * 64], cqn[:, j, :], j == 0, j == 3,
                        reads=[("wrs",), ("cqn", j)], writes=[("ps", "R", 2 * (ir % 2) + 1)])
                rs = rst[hd % 2]
                rope_combine(S, rs[:, :], pr[0:64, :], prs[0:64, :], cs[:, 0, :], cs[:, 1, :], t1[:, :], t2[:, :],
                             ("ps", "R", 2 * (ir % 2)), ("ps", "R", 2 * (ir % 2) + 1), [("rst", hd % 2)], scale=MLA_SCALE, csres=("cs", par))
                S.dma("sp", lambda e, o=qrT[hd][:, ts:ts + 512], i=rs[:, :]: e.dma_start(out=o, in_=i),
                      reads=[("rst", hd % 2)], writes=[("dram", "qrT", hd, tb)])
                ir += 1
            cnt["it"] = it
            cnt["ir"] = ir

        skew(ntok // 512, [stP, stM])
        S.flush()


def mla_attn(nc, S, C, nq, nkeys, qnT, qrT, KT, Vtok, krT, cmask_d, pmask, oT):
    npk = (nkeys - nq) // 512
    NB = nkeys // 128
    with ExitStack() as st:
        psS = [st.enter_context(_pst(nc, "ma_psS%d" % i, [128, 512], F32)) for i in range(3)]
        psT = [st.enter_context(_pst(nc, "ma_psT%d" % i, [128, 8, 128], BF16)) for i in range(2)]
        psO_ = [st.enter_context(_pst(nc, "ma_psO%d" % i, [128, 512], F32)) for i in range(2)]
        psO = [t[:, 0:128] for t in psO_]
        psX_ = st.enter_context(_pst(nc, "ma_psX", [128, 1024], BF16))
        psX = psX_[:, 0:128]
        kt = [st.enter_context(_sb(nc, "ma_kt%d" % i, [128, nkeys], BF16)) for i in range(2)]
        vt = [st.enter_context(_sb(nc, "ma_vt%d" % i, [128, NB, 128], BF16)) for i in range(2)]
        qn = [st.enter_context(_sb(nc, "ma_qn%d" % i, [128, nq], BF16)) for i in range(2)]
        qr = [st.enter_context(_sb(nc, "ma_qr%d" % i, [64, nq], BF16)) for i in range(2)]
        kr = st.enter_context(_sb(nc, "ma_kr", [64, nkeys], BF16))
        cm = st.enter_context(_sb(nc, "ma_cm", [128, 4, 512], F32))
        Ssb = [st.enter_context(_sb(nc, "ma_S%d" % i, [128, nkeys], F32)) for i in range(2)]
        Pb = [st.enter_context(_sb(nc, "ma_P%d" % i, [128, nkeys], BF16)) for i in range(2)]
        PT = [st.enter_context(_sb(nc, "ma_PT%d" % i, [128, NB, 128], BF16)) for i in range(2)]
        sv = [st.enter_context(_sb(nc, "ma_sv%d" % i, [128, 8], F32)) for i in range(4)]
        ot = [st.enter_context(_sb(nc, "ma_ot%d" % i, [128, 128], BF16)) for i in range(2)]
        oTh = [st.enter_context(_sb(nc, "ma_oTh%d" % i, [128, nq], BF16)) for i in range(2)]
        ident = C["ident_bf"]
        S.dma("sp", lambda e: e.dma_start(out=kr[:, :], in_=krT[:, :]), reads=[("dram", "krT")], writes=[("kr",)])
        S.dma("sp", lambda e: e.dma_start(out=cm[:, :, :], in_=cmask_d[:, :, :]), writes=[("cm",)])
        cmb = st.enter_context(_sb(nc, "ma_cmb", [128, 4, 512], BF16))
        S.op("dve", lambda e: e.tensor_copy(cmb[:, :, :], cm[:, :, :]), reads=[("cm",)], writes=[("cmb",)])
        mb = [st.enter_context(_sb(nc, "ma_mb%d" % i, [128, 8], F32)) for i in range(4)]
        NQT = nq // 128
        units = [(hd, qt) for hd in range(16) for qt in range(NQT)]
        cnt = {"si": 0, "ti": 0}

        def loads(hd):
            k2 = hd % 2
            S.dma("sp", lambda e, o=kt[k2][:, :], i=KT[hd]: e.dma_start(out=o, in_=i), reads=[("dram", "KT")], writes=[("kt", k2)])
            S.dma("sp", lambda e, o=vt[k2][:, :, :], i=Vtok[:, :, hd * 128:(hd + 1) * 128].rearrange("b p c -> p b c"):
                  e.dma_start(out=o, in_=i), reads=[("dram", "Vtok")], writes=[("vt", k2)])
            S.dma("sp", lambda e, o=qn[k2][:, :], i=qnT[hd]: e.dma_start(out=o, in_=i), reads=[("dram", "qnT")], writes=[("qn", k2)])
            S.dma("sp", lambda e, o=qr[k2][:, :], i=qrT[hd]: e.dma_start(out=o, in_=i), reads=[("dram", "qrT")], writes=[("qr", k2)])

        def stA(ui):
            hd, qt = units[ui]
            k2, k, k4 = hd % 2, ui % 2, ui % 4
            if qt == 0 and hd == 0:
                loads(0)
            if qt == 4 and hd + 1 < 16:
                loads(hd + 1)
            nk = npk + qt // 4 + 1
            qs = slice(qt * 128, (qt + 1) * 128)
            for kb in range(nk):
                si = cnt["si"]
                p = psS[si % 3]
                ks = slice(kb * 512, (kb + 1) * 512)
                diag = (kb == nk - 1)
                _mm(S, p[:], qn[k2][:, qs], kt[k2][:, ks], True, False, reads=[("qn", k2), ("kt", k2)], writes=[("ps", "S", si % 3)])
                _mm(S, p[:], qr[k2][:, qs], kr[:, ks], False, not diag, reads=[("qr", k2), ("kr",)], writes=[("ps", "S", si % 3)])
                if diag:
                    _mm(S, p[:], ident[:], cmb[:, qt % 4, :], False, True, reads=[("cmb",)], writes=[("ps", "S", si % 3)])
                sc1 = pmask[:, 0:1] if kb < npk else 0.0
                S.op("dve", lambda e, o=Ssb[k][:, ks], i=p[:], s1=sc1, a=mb[k4][:, kb:kb + 1]:
                     e.tensor_scalar(o, i, s1, None, ALU.add, ALU.max, accum_out=a),
                     reads=[("ps", "S", si % 3)], writes=[("Ssb", k, kb), ("mb", k4)])
                cnt["si"] += 1

        def stB(ui):
            hd, qt = units[ui]
            k, k4 = ui % 2, ui % 4
            nk = npk + qt // 4 + 1
            W = nk * 512
            sres = [("Ssb", k, kb) for kb in range(nk)]
            s_ = sv[k4]
            S.op("dve", lambda e, o=s_[:, 1:2], i=mb[k4][:, 0:nk]: e.tensor_reduce(o, i, AX.X, ALU.max, negate=True),
                 reads=[("mb", k4)], writes=[("sv", k4)])
            _act(S, Pb[k][:, 0:W], Ssb[k][:, 0:W], AF.Exp, reads=sres + [("sv", k4)], writes=[("Pb", k), ("sv", k4)],
                 bias=s_[:, 1:2], accum_out=s_[:, 2:3])

        def stC(ui):
            hd, qt = units[ui]
            k = ui % 2
            nk = npk + qt // 4 + 1
            for b8 in range((nk * 4 + 7) // 8):
                ti = cnt["ti"]
                pT_ = psT[ti % 2]
                nb8 = min(8, nk * 4 - b8 * 8)
                for j in range(nb8):
                    blk = b8 * 8 + j
                    S.op("pe", lambda e, o=pT_[:, j, :], i=Pb[k][:, blk * 128:(blk + 1) * 128]: e.transpose(o, i, ident[:]),
                         reads=[("Pb", k)], writes=[("ps", "T", ti % 2)])
                if ti % 2 == 0:
                    _act(S, PT[k][:, b8 * 8:b8 * 8 + nb8, :], pT_[:, 0:nb8, :], AF.Copy, reads=[("ps", "T", ti % 2)], writes=[("PT", k, b8)])
                else:
                    S.op("dve", lambda e, o=PT[k][:, b8 * 8:b8 * 8 + nb8, :], i=pT_[:, 0:nb8, :]: e.tensor_copy(o, i),
                         reads=[("ps", "T", ti % 2)], writes=[("PT", k, b8)])
                cnt["ti"] += 1

        def stD(ui):
            hd, qt = units[ui]
            k2, k, k4 = hd % 2, ui % 2, ui % 4
            nk = npk + qt // 4 + 1
            qs = slice(qt * 128, (qt + 1) * 128)
            s_ = sv[k4]
            pO = psO[k]
            nblk = nk * 4
            for blk in range(nblk):
                _mm(S, pO, PT[k][:, blk, :], vt[k2][:, blk, :], blk == 0, blk == nblk - 1,
                    reads=[("PT", k, blk // 8), ("vt", k2)], writes=[("ps", "O", k)])
            S.op("dve", lambda e, o=s_[:, 3:4], i=s_[:, 2:3]: e.reciprocal(o, i), reads=[("sv", k4)], writes=[("sv", k4)])
            S.op("dve", lambda e, o=ot[k][:, :], i=pO, r=s_[:, 3:4]: e.tensor_scalar(o, i, r, None, ALU.mult),
                 reads=[("ps", "O", k), ("sv", k4)], writes=[("ot", k)])
            S.op("pe", lambda e, i=ot[k][:, :]: e.transpose(psX, i, ident[:]), reads=[("ot", k)], writes=[("ps", "X")])
            _act(S, oTh[k2][:, qs], psX, AF.Copy, reads=[("ps", "X")], writes=[("oTh", k2)])
            if qt == NQT - 1:
                S.dma("sp", lambda e, o=oT[hd], i=oTh[k2][:, :]: e.dma_start(out=o, in_=i), reads=[("oTh", k2)], writes=[("dram", "oT", hd)])

        nU = len(units)
        for t in range(nU + 3):
            if t < nU:
                stA(t)
            if 0 <= t - 1 < nU:
                stB(t - 1)
            if 0 <= t - 2 < nU:
                stC(t - 2)
            if 0 <= t - 3 < nU:
                stD(t - 3)
        S.flush()


def proj_norm_residual(nc, S, C, h_in, h_out, ntok, xT, nk, w_t, gain, pfx):
    with ExitStack() as st:
        ps = [st.enter_context(_pst(nc, pfx + "_ps%d" % i, [128, 512], F32)) for i in range(2)]
        psS = st.enter_context(_pst(nc, pfx + "_psS", [128, 512], F32))
        N = NormTiles(nc, st, pfx, psS, psS, with_hs=False)
        xb = [st.enter_context(_sb(nc, pfx + "_xb%d" % i, [128, nk, 512], BF16)) for i in range(2)]
        w = [st.enter_context(_sb(nc, pfx + "_w%d" % i, [128, nk * 128], BF16)) for i in range(2)]
        Ysb = [st.enter_context(_sb(nc, pfx + "_Y%d" % i, [128, 16, 512], F32)) for i in range(2)]

        def stM(tb):
            ts = tb * 512
            x = xb[tb % 2]
            Y = Ysb[tb % 2]
            S.dma("sp", lambda e, o=x[:, :, :], i=xT[:, :, ts:ts + 512].rearrange("k p t -> p k t"): e.dma_start(out=o, in_=i),
                  reads=[("dram", "xT")], writes=[("xb", tb % 2)])
            for c in range(16):
                k = c % 2
                S.dma("poolq", lambda e, o=w[k][:, :], i=w_t[c]: e.dma_start(out=o, in_=i), writes=[("w", k)])
                for kc in range(nk):
                    _mm(S, ps[k][:], w[k][:, kc * 128:(kc + 1) * 128], x[:, kc, :], kc == 0, kc == nk - 1,
                        reads=[("w", k), ("xb", tb % 2)], writes=[("ps", k)])
                S.op("dve", lambda e, o=Y[:, c, :], i=ps[k][:]: e.tensor_copy(o, i), reads=[("ps", k)], writes=[("Y", tb % 2, c)])
                y_stats(S, N, C, Y[:, c, :], [("Y", tb % 2, c)], c, 16)
            post_rstd(S, N, tb % 2, float(D), 1.0)

        def stR(tb):
            postnorm_residual512(S, N, tb % 2, Ysb[tb % 2], lambda c: [("Y", tb % 2, c)], gain, h_in, h_out, tb * 512)

        skew(ntok // 512, [stM, stR])
        S.flush()


NG = 288

W_SPECS = [
    ("f1a_gu", [FC, 128, 4096]), ("f1a_d", [16, 128, 5632]), ("f2a_gu", [FC, 128, 4096]), ("f2a_d", [16, 128, 5632]),
    ("f1b_gu", [FC, 128, 4096]), ("f1b_d", [16, 128, 5632]), ("f2b_gu", [FC, 128, 4096]), ("f2b_d", [16, 128, 5632]),
    ("pg0", [16, 128, 2048]), ("pp0", [128, 4096]), ("pg1", [16, 128, 2048]), ("pp1", [128, 4096]),
    ("wqkv", [72, 128, 2048]), ("awo", [128, 16384]),
    ("wdkv", [128, 16 * 640]), ("wk", [128, 4 * 2048]), ("wv", [128, 4 * 2048]),
    ("wdq", [128, 16 * 512]), ("wn", [128, 4 * 2048]), ("wr", [128, 4 * 1024]), ("wrs", [128, 4 * 1024]),
    ("bwo", [16, 128, 2048]),
]


def build_program():
    nc = bass.Bass("TRN2", target_bir_lowering=False)
    NT, NQ = SEQ, HALF
    dt = lambda n, s, d=F32: nc.dram_tensor(n, s, d, kind="ExternalInput").ap()
    xT = dt("xT", [16, 128, NT])
    p0T = dt("p0T", [2, 128, NT])
    p1T = dt("p1T", [2, 128, NQ])
    pos = dt("pos", [1, NT], I32)
    pm = dt("pm", [128, 1])
    gn = dt("gn", [128, NG])
    identd = dt("identd", [128, 128])
    ropec_d = dt("ropec", [64, 4])
    cmd = dt("cmask", [128, 4, 512])
    alibi = dt("alibi", [24, 128, 256])
    W = {n: dt(n, s) for n, s in W_SPECS}
    outT = nc.dram_tensor("outT", [16, 128, NQ], F32, kind="ExternalOutput").ap()
    hT = nc.dram_tensor("hT", [16, 128, NT], F32).ap()
    qkvT = nc.dram_tensor("qkvT", [72, 128, NT], BF16).ap()
    Og = nc.dram_tensor("Og", [3, NT, 1032], F32).ap()
    cosT = nc.dram_tensor("cosT", [64, NT], F32).ap()
    sinT = nc.dram_tensor("sinT", [64, NT], F32).ap()
    KT = nc.dram_tensor("KT", [16, 128, NT], BF16).ap()
    Vtok = nc.dram_tensor("Vtok", [NT // 128, 128, 2048], BF16).ap()
    krT = nc.dram_tensor("krT", [64, NT], BF16).ap()
    qnT = nc.dram_tensor("qnT", [16, 128, NQ], BF16).ap()
    qrT = nc.dram_tensor("qrT", [16, 64, NQ], BF16).ap()
    oT = nc.dram_tensor("oT", [16, 128, NQ], BF16).ap()
    with ExitStack() as st:
        S = Sched(nc, st)
        ones = st.enter_context(_sb(nc, "ones", [128, 128], BF16))
        ident = st.enter_context(_sb(nc, "ident", [128, 128], BF16))
        g = st.enter_context(_sb(nc, "g", [128, NG], F32))
        pmask = st.enter_context(_sb(nc, "pmask", [128, 1], F32))
        ropec = st.enter_context(_sb(nc, "ropec_sb", [64, 4], F32))
        S.op("dve", lambda e: e.memset(ones[:], 1.0), writes=[("ones",)])
        S.dma("sp", lambda e: e.dma_start(out=g[:], in_=gn[:, :]), writes=[("g",)])
        S.dma("sp", lambda e: e.dma_start(out=pmask[:], in_=pm[:, :]), writes=[("pmk",)])
        S.dma("sp", lambda e: e.dma_start(out=ropec[:], in_=ropec_d[:, :]), writes=[("rc",)])
        S.dma("poolq", lambda e: e.dma_start(out=ident[:], in_=identd[:, :]), writes=[("id",)])
        S.flush()
        C = {"ones_bf": ones, "ident_bf": ident}
        G = lambda l, n: g[:, (l * 8 + n) * 16:(l * 8 + n + 1) * 16]
        g_kvin, g_kv, g_q = g[:, 256:272], g[:, 272:276], g[:, 276:280]

        def ffn(h_in, h_out, ntok, wgu, wd, gpre, gpost):
            with ExitStack() as st2:
                Tl = FFNTiles(nc, st2)
                ffn_block(nc, S, Tl, C, h_in, h_out, ntok, wgu, wd, gpre, gpost)
                S.flush()

        rope_tables(nc, S, C, pos, NT, ropec, cosT, sinT)
        ffn(xT, hT, NT, W["f1a_gu"], W["f1a_d"], G(0, 0), G(0, 1))
        mixA_qkv(nc, S, C, hT, NT, W["wqkv"], G(0, 2), qkvT)
        mixA_attn(nc, S, C, NT, qkvT, alibi, pmask, Og)
        mixA_out(nc, S, C, hT, NT, Og, W["awo"], G(0, 3))
        ffn(hT, hT, NT, W["f2a_gu"], W["f2a_d"], G(0, 4), G(0, 5))
        ple_block(nc, S, C, hT, hT, NT, p0T, W["pg0"], W["pp0"], G(0, 6), G(0, 7))
        shared_kv(nc, S, C, hT, NT, W["wdkv"], W["wk"], W["wv"], g_kvin, g_kv, cosT, sinT, KT, Vtok, krT)
        hO = hT[:, :, NT - NQ:NT]
        ffn(hO, hO, NQ, W["f1b_gu"], W["f1b_d"], G(1, 0), G(1, 1))
        mla_q(nc, S, C, hO, NQ, NT - NQ, W["wdq"], W["wn"], W["wr"], W["wrs"], G(1, 2), g_q, cosT, sinT, qnT, qrT)
        mla_attn(nc, S, C, NQ, NT, qnT, qrT, KT, Vtok, krT, cmd, pmask, oT)
        proj_norm_residual(nc, S, C, hO, hO, NQ, oT, 16, W["bwo"], G(1, 3), "mo")
        ffn(hO, hO, NQ, W["f2b_gu"], W["f2b_d"], G(1, 4), G(1, 5))
        ple_block(nc, S, C, hO, outT, NQ, p1T, W["pg1"], W["pp1"], G(1, 6), G(1, 7))
    return nc


def _tile_cols(w, width=128):
    K, Nc = w.shape
    return np.ascontiguousarray(w.reshape(K // 128, 128, Nc // width, width).transpose(2, 1, 0, 3)).reshape(
        Nc // width, 128, (K // 128) * width)


def _tile_rows(w):
    K, Nc = w.shape
    return np.ascontiguousarray(w.reshape(K // 128, 128, Nc).transpose(1, 0, 2)).reshape(128, (K // 128) * Nc)


def _tile_wgu(wg, wu):
    a = wg.reshape(16, 128, FC, 128).transpose(2, 1, 0, 3)
    b = wu.reshape(16, 128, FC, 128).transpose(2, 1, 0, 3)
    return np.ascontiguousarray(np.stack([a, b], axis=2)).reshape(FC, 128, 4096)


def _tile_wd(wd):
    return np.ascontiguousarray(wd.reshape(FC, 128, 16, 128).transpose(2, 1, 0, 3)).reshape(16, 128, 5632)


def _gl(gv):
    return np.ascontiguousarray(np.asarray(gv, np.float32).reshape(-1, 128).T)


def _fm(a):
    t, f = a.shape
    return np.ascontiguousarray(a.T).reshape(f // 128, 128, t)


def _const_tables():
    slopes = 2.0 ** (-8.0 * np.arange(1, 25) / 24)
    q = np.arange(128)[:, None]
    k = np.arange(256)[None, :]
    diff = 128 + q - k
    valid = (diff >= 0) & (diff <= 128)
    alibi = np.zeros((24, 128, 256), np.float32)
    for gi in range(3):
        for h in range(8):
            alibi[gi * 8 + h] = np.where(valid, -(slopes[gi * 8 + h] * A_DIL[gi]) * diff, NEGB)
    pidx = np.arange(64)
    ropec = np.stack([10000.0 ** (-(2.0 * (pidx % 32)) / 64), np.where(pidx < 32, -1.0, 1.0),
                      np.full(64, -np.pi), np.zeros(64)], axis=1).astype(np.float32)
    q_ = np.arange(128)[:, None, None]
    v_ = np.arange(4)[None, :, None]
    kk = np.arange(512)[None, None, :]
    cmask = np.where(kk <= v_ * 128 + q_, 0.0, NEGB).astype(np.float32)
    return alibi, ropec, cmask


def prepare_inputs(x, p, positions, norms, ffn1_wg, ffn1_wu, ffn1_wd, ffn2_wg, ffn2_wu, ffn2_wd,
                   ple_proj, ple_gate, a_wqkv, a_wo, b_wdq, b_q_norm, b_wuq, b_wo,
                   kv_in_norm, w_dkv, kv_norm, w_ukv, cores=range(8)):
    f32 = lambda a: np.asarray(a, np.float32)
    x, p = f32(x), f32(p)
    positions = np.asarray(positions, np.int32)
    norms = f32(norms)
    alibi, ropec, cmask = _const_tables()
    shared = {"identd": np.eye(128, dtype=np.float32), "ropec": ropec, "cmask": cmask, "alibi": alibi}
    gcols = [_gl(norms[l, n]) for l in range(2) for n in range(8)]
    gcols += [_gl(kv_in_norm), _gl(kv_norm), _gl(b_q_norm), np.zeros((128, NG - 280), np.float32)]
    shared["gn"] = np.ascontiguousarray(np.concatenate(gcols, axis=1))
    shared["f1a_gu"] = _tile_wgu(f32(ffn1_wg[0]), f32(ffn1_wu[0])); shared["f1a_d"] = _tile_wd(f32(ffn1_wd[0]))
    shared["f2a_gu"] = _tile_wgu(f32(ffn2_wg[0]), f32(ffn2_wu[0])); shared["f2a_d"] = _tile_wd(f32(ffn2_wd[0]))
    shared["f1b_gu"] = _tile_wgu(f32(ffn1_wg[1]), f32(ffn1_wu[1])); shared["f1b_d"] = _tile_wd(f32(ffn1_wd[1]))
    shared["f2b_gu"] = _tile_wgu(f32(ffn2_wg[1]), f32(ffn2_wu[1])); shared["f2b_d"] = _tile_wd(f32(ffn2_wd[1]))
    for l in range(2):
        shared["pg%d" % l] = _tile_cols(f32(ple_gate[l]))
        shared["pp%d" % l] = _tile_rows(f32(ple_proj[l]))
    shared["wqkv"] = _tile_cols(f32(a_wqkv[0]))
    shared["awo"] = np.ascontiguousarray(f32(a_wo[0]).reshape(8, 128, 16, 128).transpose(1, 2, 0, 3)).reshape(128, 16384)
    wd_ = f32(w_dkv)
    rp = wd_[:, 512:576]
    shared["wdkv"] = _tile_rows(np.concatenate([wd_, rp[:, 32:], rp[:, :32]], axis=1))
    wu = f32(w_ukv).reshape(512, 16, 256)
    shared["wk"] = _tile_rows(np.ascontiguousarray(wu[:, :, :128]).reshape(512, 2048))
    shared["wv"] = _tile_rows(np.ascontiguousarray(wu[:, :, 128:]).reshape(512, 2048))
    shared["wdq"] = _tile_rows(f32(b_wdq[0]))
    wq = f32(b_wuq[0]).reshape(512, 16, 192)
    wqr = wq[:, :, 128:]
    wqrs = np.concatenate([wqr[:, :, 32:], wqr[:, :, :32]], axis=2)
    shared["wn"] = _tile_rows(np.ascontiguousarray(wq[:, :, :128]).reshape(512, 2048))
    shared["wr"] = _tile_rows(np.ascontiguousarray(wqr).reshape(512, 1024))
    shared["wrs"] = _tile_rows(np.ascontiguousarray(wqrs).reshape(512, 1024))
    shared["bwo"] = _tile_cols(f32(b_wo[0]))
    in_maps = []
    for c in cores:
        b, half = c // 2, c % 2
        sel = np.concatenate([np.arange(0, HALF), np.arange(half * HALF, (half + 1) * HALF)])
        m = dict(shared)
        m["xT"] = _fm(x[b][sel])
        m["p0T"] = _fm(p[0, b][sel])
        m["p1T"] = _fm(p[1, b][half * HALF:(half + 1) * HALF])
        m["pos"] = np.ascontiguousarray(positions[b][sel][None, :])
        m["pm"] = np.full((128, 1), 0.0 if half == 1 else NEGB, np.float32)
        in_maps.append(m)
    return in_maps


def kernel(**inputs):
    in_maps = prepare_inputs(**inputs)
    nc = build_program()
    res = run_bass_kernel_spmd(nc, in_maps, core_ids=list(range(8)))
    out = np.empty((4, SEQ, D), np.float32)
    for c in range(8):
        b, half = c // 2, c % 2
        oT_ = np.asarray(res.results[c]["outT"], np.float32).reshape(D, HALF)
        out[b, half * HALF:(half + 1) * HALF, :] = oT_.T
    return out
```

```python
import numpy as np
from contextlib import ExitStack
import concourse.bass as bass
import concourse.mybir as mybir
from concourse.bass_utils import run_bass_kernel_spmd

F32 = mybir.dt.float32
BF16 = mybir.dt.bfloat16
I32 = mybir.dt.int32
AF = mybir.ActivationFunctionType
ALU = mybir.AluOpType
AX = mybir.AxisListType

D = 2048
KC = 16
DFF = 5632
FC = 44
SEQ = 4096
HALF = 2048
EPS = 1e-6
NEGB = -30000.0


_UID = [0]


def _sb(nc, name, shape, dtype):
    _UID[0] += 1
    return nc.sbuf_tensor("%s_%d" % (name, _UID[0]), shape, dtype)


def _pst(nc, name, shape, dtype):
    _UID[0] += 1
    return nc.psum_tensor("%s_%d" % (name, _UID[0]), shape, dtype)


class Sched:
    NDMA = 8

    def __init__(self, nc, stack):
        self.nc = nc
        self.esem = {}
        for e in ("pe", "act", "dve", "pool"):
            self.esem[e] = stack.enter_context(nc.semaphore("s_" + e))
        self.dsem = {}
        for q in ("sp", "poolq"):
            self.dsem[q] = [stack.enter_context(nc.semaphore("d_%s%d" % (q, i))) for i in range(self.NDMA)]
        self.ecount = {e: 0 for e in self.esem}
        self.dcount = {q: 0 for q in self.dsem}
        self.waited = {}
        self.ops = []

    def op(self, eng, fn, reads=(), writes=()):
        writes = tuple(writes) + tuple(r for r in reads if isinstance(r, tuple) and r and r[0] == "ps")
        self.ops.append((eng, fn, tuple(reads), writes, False))

    def dma(self, q, fn, reads=(), writes=()):
        self.ops.append((q, fn, tuple(reads), tuple(writes), True))

    def flush(self):
        nc = self.nc
        ops = self.ops
        self.ops = []
        n = len(ops)
        deps = [None] * n
        signal = [False] * n
        last_writer = {}
        readers = {}
        dma_hist = {q: [] for q in self.dsem}
        for i, (eng, fn, rds, wrs, isdma) in enumerate(ops):
            d = set()
            for r in rds:
                w = last_writer.get(r)
                if w is not None:
                    d.add(w)
            for w_ in wrs:
                w = last_writer.get(w_)
                if w is not None:
                    d.add(w)
                rd = readers.get(w_)
                if rd:
                    d.update(rd.values())
            if isdma:
                hist = dma_hist[eng]
                if len(hist) >= self.NDMA:
                    d.add(hist[-self.NDMA])
                hist.append(i)
            d.discard(i)
            if eng == "pe":
                d = {x for x in d if ops[x][0] != "pe"}
            deps[i] = d
            for x in d:
                signal[x] = True
            for r in rds:
                rd = readers.setdefault(r, {})
                rd[("dma", i) if isdma else eng] = i
            for w_ in wrs:
                last_writer[w_] = i
                readers[w_] = {}
        info = [None] * n
        for i, (eng, fn, rds, wrs, isdma) in enumerate(ops):
            if isdma:
                k = self.dcount[eng]
                self.dcount[eng] += 1
                info[i] = (self.dsem[eng][k % self.NDMA], 16 * (k // self.NDMA + 1), ("d", eng, k % self.NDMA))
            elif signal[i]:
                self.ecount[eng] += 1
                info[i] = (self.esem[eng], self.ecount[eng], ("e", eng))
        engs = {"pe": "tensor", "act": "scalar", "dve": "vector", "pool": "gpsimd", "sp": "sync", "poolq": "gpsimd"}
        streams = {"tensor": [], "scalar": [], "vector": [], "gpsimd": [], "sync": []}
        for i, o in enumerate(ops):
            streams[engs[o[0]]].append(i)
        end_waits = []
        for q in self.dsem:
            k = self.dcount[q]
            for s in range(self.NDMA):
                cnt = (k - s + self.NDMA - 1) // self.NDMA if k > s else 0
                if cnt > 0:
                    end_waits.append((self.dsem[q][s], 16 * cnt, ("d", q, s)))

        def emit_stream(sname):
            def body(e):
                wt = self.waited.setdefault(sname, {})
                for i in streams[sname]:
                    eng, fn, rds, wrs, isdma = ops[i]
                    need = {}
                    for x in deps[i]:
                        s, val, key = info[x]
                        if key not in need or need[key][1] < val:
                            need[key] = (s, val)
                    for key, (s, val) in need.items():
                        if wt.get(key, 0) >= val:
                            continue
                        e.wait_ge(s, val)
                        wt[key] = val
                    ins = fn(e)
                    if isdma:
                        ins.then_inc(info[i][0], 16)
                    elif signal[i]:
                        ins.then_inc(info[i][0], 1)
                if sname == "sync":
                    for s, v, key in end_waits:
                        if wt.get(key, 0) < v:
                            e.wait_ge(s, v)
                            wt[key] = v
            return body

        with nc.Block() as block:
            block.tensor(emit_stream("tensor"))
            block.scalar(emit_stream("scalar"))
            block.vector(emit_stream("vector"))
            block.gpsimd(emit_stream("gpsimd"))
            block.sync(emit_stream("sync"))
        return n


def _mm(S, out, lhsT, rhs, start, stop, reads, writes):
    S.op("pe", lambda e, a=out, l=lhsT, r=rhs, st=start, sp=stop: e.matmul(a, l, r, start=st, stop=sp),
         reads, writes)


def _act(S, out, in_, func, reads, writes, bias=None, scale=None, accum_out=None):
    kw = {}
    if bias is not None:
        kw["bias"] = bias
    if scale is not None:
        kw["scale"] = scale
    if accum_out is not None:
        kw["accum_out"] = accum_out
    S.op("act", lambda e, o=out, i=in_, f=func, k=kw: e.activation(o, i, f, **k), reads, writes)


def _rstd_from_ss(S, rstd, ss_ps, nfeat, mult, reads, writes, eps_ap=None):
    m2 = 1.0 / (mult * mult)
    S.op("dve", lambda e, o=rstd, i=ss_ps: e.tensor_scalar(o, i, m2 / nfeat, EPS * m2, ALU.mult, ALU.add),
         reads, writes)
    S.op("act", lambda e, o=rstd: e.activation(o, o, AF.Sqrt), writes, writes)
    S.op("dve", lambda e, o=rstd: e.reciprocal(o, o), writes, writes)


class FFNTiles:
    def __init__(self, nc, stack):
        self.AT = stack.enter_context(_sb(nc, "ffn_AT", [128, FC, 1024], BF16))
        self.R = stack.enter_context(_sb(nc, "ffn_R", [128, 32768], BF16))
        self.wb = [stack.enter_context(_sb(nc, "ffn_wb%d" % i, [128, 5632], BF16)) for i in range(2)]
        self.sq = [stack.enter_context(_sb(nc, "ffn_sq%d" % i, [128, 512], BF16)) for i in range(2)]
        self.sg = [stack.enter_context(_sb(nc, "ffn_sg%d" % i, [128, 512], BF16)) for i in range(2)]
        self.hr = [stack.enter_context(_sb(nc, "ffn_hr%d" % i, [128, 512], F32)) for i in range(6)]
        self.tmp = [stack.enter_context(_sb(nc, "ffn_tmp%d" % i, [128, 512], F32)) for i in range(4)]
        self.rstd = [stack.enter_context(_sb(nc, "ffn_rstd%d" % i, [128, 512], F32)) for i in range(2)]
        self.rstdp = [stack.enter_context(_sb(nc, "ffn_rstdp%d" % i, [128, 512], F32)) for i in range(2)]
        self.ps = [stack.enter_context(_pst(nc, "ffn_ps%d" % i, [128, 512], F32)) for i in range(8)]
        self.Ysb = self.R[:, 0:16384].rearrange("p (c t) -> p c t", c=16)
        self.xnT = self.R[:, 16384:32768].rearrange("p (c t) -> p c t", c=16)


def ffn_block(nc, S, Tl, C, h_in, h_out, ntok, wgu_t, wd_t, gpre, gpost):
    ones = C["ones_bf"]
    psG = [Tl.ps[0], Tl.ps[2]]
    psU = [Tl.ps[1], Tl.ps[3]]
    psY = [Tl.ps[4], Tl.ps[5]]
    psS = [Tl.ps[6], Tl.ps[7]]
    st_ = {"sq": 0, "hr": 0, "tm": 0}

    def hload(c, ts):
        slot = st_["hr"] % 6
        st_["hr"] += 1
        S.dma("sp", lambda e, o=Tl.hr[slot][:], i=h_in[c, :, ts:ts + 512]: e.dma_start(out=o, in_=i),
              reads=[("dram", "h")], writes=[("hr", slot)])
        return slot

    def prenorm_gen(t0):
        for tb in range(2):
            ts = t0 + tb * 512
            pend = []
            for c in range(16 + 4):
                if c < 16:
                    pend.append((c, hload(c, ts)))
                if c >= 4:
                    cc, slot = pend.pop(0)
                    k = st_["sq"] % 2
                    st_["sq"] += 1
                    _act(S, Tl.sq[k][:], Tl.hr[slot][:], AF.Square, reads=[("hr", slot)], writes=[("sq", k)])
                    _mm(S, Tl.ps[tb][:], ones[:], Tl.sq[k][:], cc == 0, cc == 15, reads=[("sq", k)], writes=[("ps", tb)])
                    yield
            _rstd_from_ss(S, Tl.rstdp[tb][:], Tl.ps[tb][:], float(D), 1.0, reads=[("ps", tb)], writes=[("rstdp", tb)])
            yield
        for tb in range(2):
            ts = t0 + tb * 512
            pend = []
            for c in range(16 + 4):
                if c < 16:
                    pend.append((c, hload(c, ts)))
                if c >= 4:
                    cc, slot = pend.pop(0)
                    S.op("dve", lambda e, o=Tl.xnT[:, cc, tb * 512:(tb + 1) * 512], i=Tl.hr[slot][:], g=gpre[:, cc:cc + 1],
                         r=Tl.rstdp[tb][:]: e.scalar_tensor_tensor(o, i, g, r, ALU.mult, ALU.mult),
                         reads=[("hr", slot), ("rstdp", tb)], writes=[("xn", cc, tb)])
                    yield

    def post_gen(t0):
        for tb in range(2):
            ts = t0 + tb * 512
            _rstd_from_ss(S, Tl.rstd[tb][:], psS[tb][:], float(D), 0.5, reads=[("ps", 6 + tb)], writes=[("rstd", tb)])
            yield
            pend = []
            for c in range(16 + 4):
                if c < 16:
                    pend.append((c, hload(c, ts)))
                if c >= 4:
                    cc, slot = pend.pop(0)
                    k = st_["tm"] % 4
                    st_["tm"] += 1
                    hr, tmp = Tl.hr[slot], Tl.tmp[k]
                    S.op("dve", lambda e, o=tmp[:], i=Tl.Ysb[:, cc, tb * 512:(tb + 1) * 512], g=gpost[:, cc:cc + 1], r=Tl.rstd[tb][:]:
                         e.scalar_tensor_tensor(o, i, g, r, ALU.mult, ALU.mult),
                         reads=[("Y", cc, tb), ("rstd", tb)], writes=[("tmp", k)])
                    S.op("dve", lambda e, o=hr[:], a=hr[:], b=tmp[:]: e.tensor_tensor(o, a, b, ALU.add),
                         reads=[("tmp", k), ("hr", slot)], writes=[("hr", slot)])
                    S.dma("sp", lambda e, o=h_out[cc, :, ts:ts + 512], i=hr[:]: e.dma_start(out=o, in_=i),
                          reads=[("hr", slot)], writes=[("dram", "hout", cc, ts)])
                    yield

    def drain(g, n=None):
        if g is None:
            return None
        try:
            if n is None:
                while True:
                    next(g)
            else:
                for _ in range(n):
                    next(g)
        except StopIteration:
            return None
        return g

    npass = ntok // 1024
    drain(prenorm_gen(0))
    postg = None
    for p in range(npass):
        t0 = p * 1024
        it = 0
        for f in range(FC):
            wb = Tl.wb[f % 2]
            S.dma("poolq", lambda e, o=wb[:, 0:4096], i=wgu_t[f]: e.dma_start(out=o, in_=i),
                  reads=[], writes=[("wb", f % 2)])
            for tb in range(2):
                pg, pu = psG[it % 2], psU[it % 2]
                for kc in range(16):
                    _mm(S, pg[:], wb[:, kc * 128:(kc + 1) * 128], Tl.xnT[:, kc, tb * 512:(tb + 1) * 512], kc == 0, kc == 15,
                        reads=[("wb", f % 2), ("xn", kc, tb)], writes=[("ps", 2 * (it % 2))])
                for kc in range(16):
                    _mm(S, pu[:], wb[:, 2048 + kc * 128:2048 + (kc + 1) * 128], Tl.xnT[:, kc, tb * 512:(tb + 1) * 512],
                        kc == 0, kc == 15, reads=[("wb", f % 2), ("xn", kc, tb)], writes=[("ps", 2 * (it % 2) + 1)])
                sg = Tl.sg[it % 2]
                _act(S, sg[:], pg[:], AF.Silu, reads=[("ps", 2 * (it % 2))], writes=[("sg", it % 2)])
                S.op("dve", lambda e, o=Tl.AT[:, f, tb * 512:(tb + 1) * 512], a=sg[:], b=pu[:]: e.tensor_tensor(o, a, b, ALU.mult),
                     reads=[("sg", it % 2), ("ps", 2 * (it % 2) + 1)], writes=[("AT", f, tb)])
                it += 1
            if f >= 2:
                postg = drain(postg, 1)
        postg = drain(postg)
        preg = prenorm_gen(t0 + 1024) if p + 1 < npass else None
        it = 0
        for c in range(16):
            wb = Tl.wb[c % 2]
            S.dma("poolq", lambda e, o=wb[:, :], i=wd_t[c]: e.dma_start(out=o, in_=i), reads=[], writes=[("wb", c % 2)])
            for tb in range(2):
                py = psY[it % 2]
                for kc in range(FC):
                    _mm(S, py[:], wb[:, kc * 128:(kc + 1) * 128], Tl.AT[:, kc, tb * 512:(tb + 1) * 512], kc == 0, kc == FC - 1,
                        reads=[("wb", c % 2), ("AT", kc, tb)], writes=[("ps", 4 + it % 2)])
                k = st_["sq"] % 2
                st_["sq"] += 1
                sq = Tl.sq[k]
                _act(S, sq[:], py[:], AF.Square, reads=[("ps", 4 + it % 2)], writes=[("sq", k)])
                S.op("dve", lambda e, o=Tl.Ysb[:, c, tb * 512:(tb + 1) * 512], i=py[:]: e.tensor_copy(o, i),
                     reads=[("ps", 4 + it % 2)], writes=[("Y", c, tb)])
                _mm(S, psS[tb][:], ones[:], sq[:], c == 0, c == 15, reads=[("sq", k)], writes=[("ps", 6 + tb)])
                it += 1
                if c >= 1:
                    preg = drain(preg, 3)
        preg = drain(preg)
        postg = post_gen(t0)
    drain(postg)


class NormTiles:
    def __init__(self, nc, stack, pfx, ps_pre, ps_post, with_hs=True, with_hb=True):
        if with_hs:
            self.hs = stack.enter_context(_sb(nc, pfx + "_hs", [128, 16, 512], F32))
        self.sq = [stack.enter_context(_sb(nc, pfx + "_sq%d" % i, [128, 512], BF16)) for i in range(4)]
        self.rstd_pre = stack.enter_context(_sb(nc, pfx + "_rstdp", [128, 512], F32))
        self.rstd_post = [stack.enter_context(_sb(nc, pfx + "_rstdq%d" % i, [128, 512], F32)) for i in range(2)]
        self.with_hb = with_hb
        if with_hb:
            self.hb = stack.enter_context(_sb(nc, pfx + "_hb", [128, 16, 512], F32))
        self.tmp = [stack.enter_context(_sb(nc, pfx + "_tmp%d" % i, [128, 512], F32)) for i in range(2)]
        self.ps_pre = ps_pre
        self.ps_post = ps_post
        self.sqc = 0
        self.sqd = 0
        self.itc = 0


def prenorm512(S, N, C, h_in, ts, gain, out_fn):
    ones = C["ones_bf"]
    for c4 in range(4):
        src = h_in[c4 * 4:(c4 + 1) * 4, :, ts:ts + 512].rearrange("c p t -> p c t")
        S.dma("sp", lambda e, o=N.hs[:, c4 * 4:(c4 + 1) * 4, :], i=src: e.dma_start(out=o, in_=i),
              reads=[("dram", "h")], writes=[("hs", c4 * 4 + j) for j in range(4)])
    for c in range(16):
        k = N.sqc % 2
        _act(S, N.sq[k][:], N.hs[:, c, :], AF.Square, reads=[("hs", c)], writes=[("nsq", k)])
        _mm(S, N.ps_pre[:], ones[:], N.sq[k][:], c == 0, c == 15, reads=[("nsq", k)], writes=[("ps", "ssP")])
        N.sqc += 1
    _rstd_from_ss(S, N.rstd_pre[:], N.ps_pre[:], float(D), 1.0, reads=[("ps", "ssP")], writes=[("nrstdP",)])
    for c in range(16):
        o, wr = out_fn(c)
        S.op("dve", lambda e, o=o, i=N.hs[:, c, :], g=gain[:, c:c + 1], r=N.rstd_pre[:]:
             e.scalar_tensor_tensor(o, i, g, r, ALU.mult, ALU.mult),
             reads=[("hs", c), ("nrstdP",)], writes=wr)


def y_stats(S, N, C, ysb_c, yres, c, nchunks):
    k = 2 + N.sqd % 2
    _act(S, N.sq[k][:], ysb_c, AF.Square, reads=yres, writes=[("nsq", k)])
    _mm(S, N.ps_post[:], C["ones_bf"][:], N.sq[k][:], c == 0, c == nchunks - 1, reads=[("nsq", k)], writes=[("ps", "ssM")])
    N.sqd += 1


def post_rstd(S, N, par, nfeat, mult):
    _rstd_from_ss(S, N.rstd_post[par][:], N.ps_post[:], nfeat, mult, reads=[("ps", "ssM")], writes=[("nrstdQ", par)])


def postnorm_residual512(S, N, par, ysb, yres_fn, gain, h_in, h_out, ts):
    for c4 in range(4):
        src = h_in[c4 * 4:(c4 + 1) * 4, :, ts:ts + 512].rearrange("c p t -> p c t")
        S.dma("sp", lambda e, o=N.hb[:, c4 * 4:(c4 + 1) * 4, :], i=src: e.dma_start(out=o, in_=i),
              reads=[("dram", "h")], writes=[("nhb", c4 * 4 + j) for j in range(4)])
    for c in range(16):
        k = N.itc % 2
        tmp = N.tmp[k]
        S.op("dve", lambda e, o=tmp[:], i=ysb[:, c, :], g=gain[:, c:c + 1], r=N.rstd_post[par][:]:
             e.scalar_tensor_tensor(o, i, g, r, ALU.mult, ALU.mult),
             reads=list(yres_fn(c)) + [("nrstdQ", par)], writes=[("ntmp", k)])
        S.op("dve", lambda e, o=N.hb[:, c, :], a=N.hb[:, c, :], b=tmp[:]: e.tensor_tensor(o, a, b, ALU.add),
             reads=[("ntmp", k), ("nhb", c)], writes=[("nhb", c)])
        N.itc += 1
        if c % 4 == 3:
            c4 = c // 4
            dst = h_out[c4 * 4:(c4 + 1) * 4, :, ts:ts + 512].rearrange("c p t -> p c t")
            S.dma("sp", lambda e, o=dst, i=N.hb[:, c4 * 4:(c4 + 1) * 4, :]: e.dma_start(out=o, in_=i),
                  reads=[("nhb", c4 * 4 + j) for j in range(4)], writes=[("dram", "hout", c4, ts)])


def skew(n, stages):
    ns = len(stages)
    for t in range(n + ns - 1):
        for s, fn in enumerate(stages):
            i = t - s
            if 0 <= i < n:
                fn(i)


A_DIL = (1, 4, 16)


def mixA_qkv(nc, S, C, hT, ntok, wqkv_t, gain, qkvT):
    with ExitStack() as st:
        ps = [st.enter_context(_pst(nc, "qa_ps%d" % i, [128, 512], F32)) for i in range(3)]
        N = NormTiles(nc, st, "qa", ps[2], ps[2], with_hb=False)
        hnT = st.enter_context(_sb(nc, "qa_hnT", [128, 16, 2048], BF16))
        wq = [st.enter_context(_sb(nc, "qa_w%d" % i, [128, 2048], BF16)) for i in range(2)]
        stg = [st.enter_context(_sb(nc, "qa_stg%d" % i, [128, 2048], BF16)) for i in range(2)]
        nhalf = ntok // 2048
        for hf in range(nhalf):
            for tb in range(4):
                prenorm512(S, N, C, hT, hf * 2048 + tb * 512, gain,
                           lambda c, tb=tb: (hnT[:, c, tb * 512:(tb + 1) * 512], [("hnT", c, tb)]))
            it = 0
            for ch in range(72):
                s_, g_ = ch // 24, (ch // 8) % 3
                d = A_DIL[g_]
                L = ntok // d
                Lh = 2048 // d
                w = wq[ch % 2]
                S.dma("poolq", lambda e, o=w[:, :], i=wqkv_t[ch]: e.dma_start(out=o, in_=i), writes=[("qw", ch % 2)])
                sg = stg[ch % 2]
                sgv = sg[:, :].rearrange("p (r j) -> p r j", r=d)
                for tb in range(4):
                    p = ps[it % 2]
                    for kc in range(16):
                        _mm(S, p[:], w[:, kc * 128:(kc + 1) * 128], hnT[:, kc, tb * 512:(tb + 1) * 512], kc == 0, kc == 15,
                            reads=[("qw", ch % 2), ("hnT", kc, tb)], writes=[("ps", it % 2)])
                    jw = 512 // d
                    dst = sgv[:, :, tb * jw:(tb + 1) * jw]
                    src = p[:, :].rearrange("p (j r) -> p r j", r=d)
                    sc = 128.0 ** -0.5 if s_ == 0 else 1.0
                    if it % 2 == 0:
                        _act(S, dst, src, AF.Copy, reads=[("ps", it % 2)], writes=[("qstg", ch % 2)], scale=sc)
                    else:
                        S.op("dve", lambda e, o=dst, i=src, sc=sc: e.tensor_scalar(o, i, sc, None, ALU.mult),
                             reads=[("ps", it % 2)], writes=[("qstg", ch % 2)])
                    it += 1
                dd = qkvT[ch].rearrange("p (r j) -> p r j", r=d)[:, :, hf * Lh:(hf + 1) * Lh]
                S.dma("sp", lambda e, o=dd, i=sgv: e.dma_start(out=o, in_=i),
                      reads=[("qstg", ch % 2)], writes=[("dram", "qkvT", ch, hf)])
        S.flush()


def mixA_attn(nc, S, C, ntok, qkvT, alibi, pmask, Og):
    NU = ntok // 128
    with ExitStack() as st:
        psS = [st.enter_context(_pst(nc, "ab_psS%d" % i, [128, 2, 256], F32)) for i in range(2)]
        psT = [st.enter_context(_pst(nc, "ab_psT%d" % i, [128, 8, 128], BF16)) for i in range(2)]
        psO = [st.enter_context(_pst(nc, "ab_psO%d" % i, [128, 4, 128], F32)) for i in range(2)]
        psV = [st.enter_context(_pst(nc, "ab_psV%d" % i, [128, 8, 128], BF16)) for i in range(2)]
        qT = [st.enter_context(_sb(nc, "ab_q%d" % i, [128, ntok], BF16)) for i in range(2)]
        kT = [st.enter_context(_sb(nc, "ab_k%d" % i, [128, 128 + ntok], BF16)) for i in range(2)]
        vT = [st.enter_context(_sb(nc, "ab_v%d" % i, [128, ntok], BF16)) for i in range(2)]
        vtok = [st.enter_context(_sb(nc, "ab_vt%d" % i, [128, NU, 128], BF16)) for i in range(2)]
        bN = [st.enter_context(_sb(nc, "ab_bN%d" % i, [128, 256], F32)) for i in range(2)]
        bF = [st.enter_context(_sb(nc, "ab_bF%d" % i, [128, 256], F32)) for i in range(2)]
        Ssb = [st.enter_context(_sb(nc, "ab_S%d" % i, [128, 2, 256], F32)) for i in range(2)]
        Pb = [st.enter_context(_sb(nc, "ab_P%d" % i, [128, 2, 256], BF16)) for i in range(2)]
        PT = [st.enter_context(_sb(nc, "ab_PT%d" % i, [128, 4, 128], BF16)) for i in range(2)]
        osb = [st.enter_context(_sb(nc, "ab_o%d" % i, [128, NU, 128], F32)) for i in range(2)]
        lse = [st.enter_context(_sb(nc, "ab_lse%d" % i, [128, NU, 8], F32)) for i in range(2)]
        stt = [st.enter_context(_sb(nc, "ab_st%d" % i, [128, 16], F32)) for i in range(4)]
        ident = C["ident_bf"]
        for i in range(2):
            S.op("dve", lambda e, o=kT[i][:, 0:128]: e.memset(o, 0.0), writes=[("kT", i)])
        gh = 0
        bt = 0
        for g_ in range(3):
            d = A_DIL[g_]
            L = ntok // d
            nb = L // 128
            bO = (L // 2) // 128 if ntok == 4096 else -1
            for h_ in range(8):
                k2 = gh % 2
                chq, chk, chv = g_ * 8 + h_, 24 + g_ * 8 + h_, 48 + g_ * 8 + h_
                def loads(gh_):
                    gg, hh = gh_ // 8, gh_ % 8
                    kk2 = gh_ % 2
                    S.dma("sp", lambda e, o=qT[kk2][:, :], i=qkvT[gg * 8 + hh]: e.dma_start(out=o, in_=i),
                          reads=[("dram", "qkvT")], writes=[("qT", kk2)])
                    S.dma("sp", lambda e, o=kT[kk2][:, 128:], i=qkvT[24 + gg * 8 + hh]: e.dma_start(out=o, in_=i),
                          reads=[("dram", "qkvT")], writes=[("kT", kk2)])
                    S.dma("sp", lambda e, o=vT[kk2][:, :], i=qkvT[48 + gg * 8 + hh]: e.dma_start(out=o, in_=i),
                          reads=[("dram", "qkvT")], writes=[("vT", kk2)])
                    S.dma("sp", lambda e, o=bN[kk2][:, :], i=alibi[gg * 8 + hh]: e.dma_start(out=o, in_=i), writes=[("bN", kk2)])
                    S.op("dve", lambda e, o=bF[kk2][:, 128:256], i=bN[kk2][:, 128:256]: e.tensor_copy(o, i),
                         reads=[("bN", kk2)], writes=[("bF", kk2)])
                    S.op("dve", lambda e, o=bF[kk2][:, 0:128]: e.memset(o, NEGB), writes=[("bF", kk2)])
                if gh == 0:
                    loads(0)
                if gh + 1 < 24:
                    loads(gh + 1)
                for q4 in range(NU // 4):
                    pv = psV[q4 % 2]
                    for j in range(4):
                        blk = q4 * 4 + j
                        S.op("pe", lambda e, o=pv[:, j, :], i=vT[k2][:, blk * 128:(blk + 1) * 128]: e.transpose(o, i, ident[:]),
                             reads=[("vT", k2)], writes=[("ps", "V", q4 % 2)])
                    if q4 % 2 == 0:
                        _act(S, vtok[k2][:, q4 * 4:(q4 + 1) * 4, :], pv[:, 0:4, :], AF.Copy,
                             reads=[("ps", "V", q4 % 2)], writes=[("vtok", k2)])
                    else:
                        S.op("dve", lambda e, o=vtok[k2][:, q4 * 4:(q4 + 1) * 4, :], i=pv[:, 0:4, :]: e.tensor_copy(o, i),
                             reads=[("ps", "V", q4 % 2)], writes=[("vtok", k2)])
                nbat = NU // 2

                def stA(bi, k2=k2, nb=nb, bO=bO):
                    u0 = bi * 2
                    k = (bt0 + bi) % 2
                    pS, sS = psS[k], Ssb[k]
                    for j in range(2):
                        u = u0 + j
                        S.op("pe", lambda e, o=pS[:, j, :], l=qT[k2][:, u * 128:(u + 1) * 128], r=kT[k2][:, u * 128:u * 128 + 256]:
                             e.matmul(o, l, r, start=True, stop=True),
                             reads=[("qT", k2), ("kT", k2)], writes=[("ps", "S", k)])
                    for j in range(2):
                        u = u0 + j
                        b = u % nb
                        bias = bF[k2] if b == 0 else bN[k2]
                        S.op("dve", lambda e, o=sS[:, j, :], a=pS[:, j, :], bb=bias[:, :]: e.tensor_tensor(o, a, bb, ALU.add),
                             reads=[("ps", "S", k), ("bN", k2), ("bF", k2)], writes=[("Ssb", k)])
                        if b == bO:
                            S.op("dve", lambda e, o=sS[:, j, 0:128]: e.tensor_scalar(o, o, pmask[:, 0:1], None, ALU.add),
                                 reads=[("Ssb", k)], writes=[("Ssb", k)])

                def stB(bi):
                    k = (bt0 + bi) % 2
                    k4 = (bt0 + bi) % 4
                    sS, pb, sv = Ssb[k], Pb[k], stt[k4]
                    S.op("dve", lambda e, o=sv[:, 2:4], i=sS[:, :, :]: e.tensor_reduce(o, i, AX.X, ALU.max, negate=True),
                         reads=[("Ssb", k)], writes=[("sv", k4)])
                    for j in range(2):
                        _act(S, pb[:, j, :], sS[:, j, :], AF.Exp, reads=[("Ssb", k), ("sv", k4)], writes=[("Pb", k), ("sv", k4)],
                             bias=sv[:, 2 + j:3 + j], accum_out=sv[:, 4 + j:5 + j])

                def stC(bi):
                    k = (bt0 + bi) % 2
                    pb, pT_, pt = Pb[k], psT[k], PT[k]
                    for j in range(2):
                        for kk in range(2):
                            S.op("pe", lambda e, o=pT_[:, j * 2 + kk, :], i=pb[:, j, kk * 128:(kk + 1) * 128]: e.transpose(o, i, ident[:]),
                                 reads=[("Pb", k)], writes=[("ps", "T", k)])
                    _act(S, pt[:, :, :], pT_[:, 0:4, :], AF.Copy, reads=[("ps", "T", k)], writes=[("PT", k)])

                def stD(bi, k2=k2, g_=g_, h_=h_):
                    u0 = bi * 2
                    k = (bt0 + bi) % 2
                    k4 = (bt0 + bi) % 4
                    pt, pO, sv = PT[k], psO[k], stt[k4]
                    for j in range(2):
                        u = u0 + j
                        for kk in range(2):
                            vb = max(u - 1 + kk, 0)
                            S.op("pe", lambda e, o=pO[:, j, :], l=pt[:, j * 2 + kk, :], r=vtok[k2][:, vb, :], st_=(kk == 0), sp_=(kk == 1):
                                 e.matmul(o, l, r, start=st_, stop=sp_),
                                 reads=[("PT", k), ("vtok", k2)], writes=[("ps", "O", k)])
                    S.op("dve", lambda e, o=sv[:, 6:8], i=sv[:, 4:6]: e.reciprocal(o, i), reads=[("sv", k4)], writes=[("sv", k4)])
                    for j in range(2):
                        S.op("dve", lambda e, o=osb[k2][:, u0 + j, :], i=pO[:, j, :], r=sv[:, 6 + j:7 + j]:
                             e.tensor_scalar(o, i, r, None, ALU.mult),
                             reads=[("ps", "O", k), ("sv", k4)], writes=[("osb", k2)])
                    _act(S, sv[:, 8:10], sv[:, 4:6], AF.Ln, reads=[("sv", k4)], writes=[("sv", k4)])
                    S.op("pool", lambda e, o=lse[g_ % 2][:, u0:u0 + 2, h_], a=sv[:, 8:10], bb=sv[:, 2:4]: e.tensor_tensor(o, a, bb, ALU.subtract),
                         reads=[("sv", k4)], writes=[("lse", g_ % 2)])

                bt0 = bt
                for t in range(nbat + 3):
                    if t < nbat:
                        stA(t)
                    if 0 <= t - 1 < nbat:
                        stB(t - 1)
                    if 0 <= t - 2 < nbat:
                        stC(t - 2)
                    if 0 <= t - 3 < nbat:
                        stD(t - 3)
                bt += nbat
                for r in range(d):
                    dst = Og[g_][:, h_ * 128:(h_ + 1) * 128].rearrange("(b i r) c -> i r b c", i=128, r=d)[:, r, :, :]
                    srcv = osb[k2][:, r * nb:(r + 1) * nb, :]
                    S.dma("sp", lambda e, o=dst, i=srcv: e.dma_start(out=o, in_=i),
                          reads=[("osb", k2)], writes=[("dram", "Og", g_, h_, r)])
                gh += 1
            for r in range(d):
                dst = Og[g_][:, 1024:1032].rearrange("(b i r) c -> i r b c", i=128, r=d)[:, r, :, :]
                srcv = lse[g_ % 2][:, r * nb:(r + 1) * nb, :]
                S.dma("sp", lambda e, o=dst, i=srcv: e.dma_start(out=o, in_=i),
                      reads=[("lse", g_ % 2)], writes=[("dram", "OgL", g_, r)])
        S.flush()


def mixA_out(nc, S, C, hT, ntok, Og, wo_t, gain):
    with ExitStack() as st:
        psT = [st.enter_context(_pst(nc, "ac_psT%d" % i, [128, 8, 128], BF16)) for i in range(2)]
        psM = [st.enter_context(_pst(nc, "ac_psM%d" % i, [128, 512], F32)) for i in range(2)]
        psS = st.enter_context(_pst(nc, "ac_psS", [128, 512], F32))
        N = NormTiles(nc, st, "ac", psS, psS, with_hs=False)
        og = [st.enter_context(_sb(nc, "ac_og%d" % i, [128, 3, 1032], F32)) for i in range(2)]
        wt = [st.enter_context(_sb(nc, "ac_w%d" % i, [128, 64], F32)) for i in range(2)]
        acc = [st.enter_context(_sb(nc, "ac_acc%d" % i, [128, 1024], F32)) for i in range(2)]
        otok = [st.enter_context(_sb(nc, "ac_ot%d" % i, [128, 1024], BF16)) for i in range(2)]
        oT = [st.enter_context(_sb(nc, "ac_oT%d" % i, [128, 8, 512], BF16)) for i in range(2)]
        wo = st.enter_context(_sb(nc, "ac_wo", [128, 16 * 8 * 128], BF16))
        Ysb = [st.enter_context(_sb(nc, "ac_Y%d" % i, [128, 16, 512], F32)) for i in range(2)]
        ident = C["ident_bf"]
        for q in range(4):
            S.dma("poolq", lambda e, o=wo[:, q * 4096:(q + 1) * 4096], i=wo_t[:, q * 4096:(q + 1) * 4096]: e.dma_start(out=o, in_=i),
                  writes=[("wo", q)])

        def stG(tb):
            oTt = oT[tb % 2]
            for tt in range(4):
                ti = tb * 4 + tt
                k = ti % 2
                t0 = tb * 512 + tt * 128
                for g_ in range(3):
                    S.dma("sp", lambda e, o=og[k][:, g_, :], i=Og[g_][t0:t0 + 128, :]: e.dma_start(out=o, in_=i),
                          reads=[("dram", "Og")], writes=[("og", k, g_)])
                w = wt[k]
                ogr = [("og", k, g_) for g_ in range(3)]
                S.op("dve", lambda e, o=w[:, 0:8], a=og[k][:, 0, 1024:1032], b=og[k][:, 1, 1024:1032]: e.tensor_tensor(o, a, b, ALU.max),
                     reads=ogr, writes=[("wt", k)])
                S.op("dve", lambda e, o=w[:, 0:8], a=w[:, 0:8], b=og[k][:, 2, 1024:1032]: e.tensor_tensor(o, a, b, ALU.max),
                     reads=ogr + [("wt", k)], writes=[("wt", k)])
                for g_ in range(3):
                    S.op("dve", lambda e, o=w[:, 8 + g_ * 8:16 + g_ * 8], a=og[k][:, g_, 1024:1032], b=w[:, 0:8]:
                         e.tensor_tensor(o, a, b, ALU.subtract), reads=ogr + [("wt", k)], writes=[("wt", k)])
                _act(S, w[:, 8:32], w[:, 8:32], AF.Exp, reads=[("wt", k)], writes=[("wt", k)])
                S.op("dve", lambda e, o=w[:, 32:40], a=w[:, 8:16], b=w[:, 16:24]: e.tensor_tensor(o, a, b, ALU.add),
                     reads=[("wt", k)], writes=[("wt", k)])
                S.op("dve", lambda e, o=w[:, 32:40], a=w[:, 32:40], b=w[:, 24:32]: e.tensor_tensor(o, a, b, ALU.add),
                     reads=[("wt", k)], writes=[("wt", k)])
                S.op("dve", lambda e, o=w[:, 40:48], i=w[:, 32:40]: e.reciprocal(o, i), reads=[("wt", k)], writes=[("wt", k)])
                for g_ in range(3):
                    S.op("dve", lambda e, o=w[:, 8 + g_ * 8:16 + g_ * 8], a=w[:, 8 + g_ * 8:16 + g_ * 8], b=w[:, 40:48]:
                         e.tensor_tensor(o, a, b, ALU.mult), reads=[("wt", k)], writes=[("wt", k)])
                for h_ in range(8):
                    hs_ = slice(h_ * 128, (h_ + 1) * 128)
                    e0 = "dve"
                    S.op(e0, lambda e, o=acc[k][:, hs_], i=og[k][:, 0, hs_], s=w[:, 8 + h_:9 + h_]: e.tensor_scalar(o, i, s, None, ALU.mult),
                         reads=ogr + [("wt", k)], writes=[("acc", k, h_)])
                    S.op(e0, lambda e, o=acc[k][:, hs_], i=og[k][:, 1, hs_], s=w[:, 16 + h_:17 + h_], a=acc[k][:, hs_]:
                         e.scalar_tensor_tensor(o, i, s, a, ALU.mult, ALU.add),
                         reads=ogr + [("wt", k), ("acc", k, h_)], writes=[("acc", k, h_)])
                    S.op(e0, lambda e, o=otok[k][:, hs_], i=og[k][:, 2, hs_], s=w[:, 24 + h_:25 + h_], a=acc[k][:, hs_]:
                         e.scalar_tensor_tensor(o, i, s, a, ALU.mult, ALU.add),
                         reads=ogr + [("wt", k), ("acc", k, h_)], writes=[("otok", k, h_)])
                pT_ = psT[k]
                for h_ in range(8):
                    S.op("pe", lambda e, o=pT_[:, h_, :], i=otok[k][:, h_ * 128:(h_ + 1) * 128]: e.transpose(o, i, ident[:]),
                         reads=[("otok", k, h_)], writes=[("ps", "T", k)])
                _act(S, oTt[:, :, tt * 128:(tt + 1) * 128], pT_[:, :, :], AF.Copy, reads=[("ps", "T", k)], writes=[("oT", tb % 2, tt)])

        def stM(tb):
            oTt = oT[tb % 2]
            Y = Ysb[tb % 2]
            for c in range(16):
                pm = psM[c % 2]
                for kc in range(8):
                    _mm(S, pm[:], wo[:, (c * 8 + kc) * 128:(c * 8 + kc + 1) * 128], oTt[:, kc, :], kc == 0, kc == 7,
                        reads=[("wo", c // 4)] + [("oT", tb % 2, tt) for tt in range(4)], writes=[("ps", "M", c % 2)])
                S.op("dve", lambda e, o=Y[:, c, :], i=pm[:]: e.tensor_copy(o, i), reads=[("ps", "M", c % 2)], writes=[("Y", tb % 2, c)])
                y_stats(S, N, C, Y[:, c, :], [("Y", tb % 2, c)], c, 16)
            post_rstd(S, N, tb % 2, float(D), 1.0)

        def stR(tb):
            postnorm_residual512(S, N, tb % 2, Ysb[tb % 2], lambda c: [("Y", tb % 2, c)], gain, hT, hT, tb * 512)

        skew(ntok // 512, [stG, stM, stR])
        S.flush()


TWO_PI = 6.283185307179586
PI = 3.141592653589793


def rope_tables(nc, S, C, pos_d, ntok, ropec, cosT, sinT):
    with ExitStack() as st:
        pi_ = st.enter_context(_sb(nc, "rp_pi", [64, 2048], I32))
        ang = st.enter_context(_sb(nc, "rp_ang", [64, 2048], F32))
        u = st.enter_context(_sb(nc, "rp_u", [64, 2048], F32))
        ki = st.enter_context(_sb(nc, "rp_ki", [64, 2048], I32))
        kf = st.enter_context(_sb(nc, "rp_kf", [64, 2048], F32))
        res = st.enter_context(_sb(nc, "rp_res", [64, 2048], F32))
        for hf in range(ntok // 2048):
            sl = slice(hf * 2048, (hf + 1) * 2048)
            S.dma("sp", lambda e, i=pos_d[0:1, sl].partition_broadcast(64): e.dma_start(out=pi_[:, :], in_=i), writes=[("rp_pi",)])
            S.op("dve", lambda e: e.tensor_copy(ang[:, :], pi_[:, :]), reads=[("rp_pi",)], writes=[("rp_ang",)])
            S.op("dve", lambda e: e.tensor_scalar(ang[:, :], ang[:, :], ropec[:, 0:1], None, ALU.mult),
                 reads=[("rp_ang",)], writes=[("rp_ang",)])
            for which, shift, dst in (("sin", 0.0, sinT), ("cos", PI / 2, cosT)):
                S.op("dve", lambda e, sh=shift: e.tensor_scalar(u[:, :], ang[:, :], sh, None, ALU.add),
                     reads=[("rp_ang",)], writes=[("rp_u",)])
                S.op("dve", lambda e: e.tensor_scalar(kf[:, :], u[:, :], 1.0 / TWO_PI, None, ALU.mult),
                     reads=[("rp_u",)], writes=[("rp_kf",)])
                S.op("dve", lambda e: e.tensor_copy(ki[:, :], kf[:, :]), reads=[("rp_kf",)], writes=[("rp_ki",)])
                S.op("dve", lambda e: e.tensor_copy(kf[:, :], ki[:, :]), reads=[("rp_ki",)], writes=[("rp_kf",)])
                S.op("dve", lambda e: e.scalar_tensor_tensor(u[:, :], kf[:, :], -TWO_PI, u[:, :], ALU.mult, ALU.add),
                     reads=[("rp_kf",), ("rp_u",)], writes=[("rp_u",)])
                S.op("dve", lambda e: e.tensor_scalar(kf[:, :], u[:, :], PI, -TWO_PI, ALU.is_gt, ALU.mult),
                     reads=[("rp_u",)], writes=[("rp_kf",)])
                S.op("dve", lambda e: e.tensor_tensor(u[:, :], u[:, :], kf[:, :], ALU.add),
                     reads=[("rp_kf",), ("rp_u",)], writes=[("rp_u",)])
                S.op("dve", lambda e: e.tensor_scalar(kf[:, :], u[:, :], -PI, TWO_PI, ALU.is_lt, ALU.mult),
                     reads=[("rp_u",)], writes=[("rp_kf",)])
                S.op("dve", lambda e: e.tensor_tensor(u[:, :], u[:, :], kf[:, :], ALU.add),
                     reads=[("rp_kf",), ("rp_u",)], writes=[("rp_u",)])
                S.op("dve", lambda e: e.tensor_scalar(u[:, :], u[:, :], PI, -PI, ALU.min, ALU.max),
                     reads=[("rp_u",)], writes=[("rp_u",)])
                _act(S, res[:, :], u[:, :], AF.Sin, reads=[("rp_u",)], writes=[("rp_res",)])
                if which == "sin":
                    S.op("dve", lambda e: e.tensor_scalar(res[:, :], res[:, :], ropec[:, 1:2], None, ALU.mult),
                         reads=[("rp_res",)], writes=[("rp_res",)])
                S.dma("sp", lambda e, o=dst[:, sl]: e.dma_start(out=o, in_=res[:, :]), reads=[("rp_res",)], writes=[("dram", which, hf)])
        S.flush()


def ple_block(nc, S, C, h_in, h_out, ntok, pT, wg_t, wp_t, gpre, gpost):
    with ExitStack() as st:
        ps = [st.enter_context(_pst(nc, "pl_ps%d" % i, [128, 512], F32)) for i in range(4)]
        psP = st.enter_context(_pst(nc, "pl_psP", [128, 512], F32))
        psQ = st.enter_context(_pst(nc, "pl_psQ", [128, 512], F32))
        N = NormTiles(nc, st, "pl", psP, psQ)
        xn = [st.enter_context(_sb(nc, "pl_xn%d" % i, [128, 16, 512], BF16)) for i in range(2)]
        pb = [st.enter_context(_sb(nc, "pl_pb%d" % i, [128, 2, 512], BF16)) for i in range(2)]
        wg = [st.enter_context(_sb(nc, "pl_wg%d" % i, [128, 2048], BF16)) for i in range(2)]
        wp = st.enter_context(_sb(nc, "pl_wp", [128, 2, 2048], BF16))
        gt = [st.enter_context(_sb(nc, "pl_gt%d" % i, [128, 512], F32)) for i in range(2)]
        Ysb = [st.enter_context(_sb(nc, "pl_Y%d" % i, [128, 16, 512], F32)) for i in range(2)]
        S.dma("poolq", lambda e: e.dma_start(out=wp[:, :, :], in_=wp_t.rearrange("p (k n) -> p k n", k=2)), writes=[("wp",)])

        def stP(tb):
            ts = tb * 512
            prenorm512(S, N, C, h_in, ts, gpre, lambda c: (xn[tb % 2][:, c, :], [("xn", tb % 2, c)]))
            S.dma("poolq", lambda e, o=pb[tb % 2][:, :, :], i=pT[:, :, ts:ts + 512].rearrange("k p t -> p k t"): e.dma_start(out=o, in_=i),
                  writes=[("pb", tb % 2)])

        def stM(tb):
            Y = Ysb[tb % 2]
            x = xn[tb % 2]
            for c in range(16):
                k = c % 2
                S.dma("poolq", lambda e, o=wg[k][:, :], i=wg_t[c]: e.dma_start(out=o, in_=i), writes=[("wg", k)])
                pg, pe_ = ps[2 * k], ps[2 * k + 1]
                for kc in range(16):
                    _mm(S, pg[:], wg[k][:, kc * 128:(kc + 1) * 128], x[:, kc, :], kc == 0, kc == 15,
                        reads=[("wg", k), ("xn", tb % 2, kc)], writes=[("ps", 2 * k)])
                for k2 in range(2):
                    _mm(S, pe_[:], wp[:, k2, c * 128:(c + 1) * 128], pb[tb % 2][:, k2, :], k2 == 0, k2 == 1,
                        reads=[("wp",), ("pb", tb % 2)], writes=[("ps", 2 * k + 1)])
                _act(S, gt[k][:], pg[:], AF.Sigmoid, reads=[("ps", 2 * k)], writes=[("gt", k)])
                S.op("dve", lambda e, o=Y[:, c, :], a=gt[k][:], b=pe_[:]: e.tensor_tensor(o, a, b, ALU.mult),
                     reads=[("gt", k), ("ps", 2 * k + 1)], writes=[("Y", tb % 2, c)])
                y_stats(S, N, C, Y[:, c, :], [("Y", tb % 2, c)], c, 16)
            post_rstd(S, N, tb % 2, float(D), 1.0)

        def stR(tb):
            postnorm_residual512(S, N, tb % 2, Ysb[tb % 2], lambda c: [("Y", tb % 2, c)], gpost, h_in, h_out, tb * 512)

        skew(ntok // 512, [stP, stM, stR])
        S.flush()


def rope_combine(S, out_bf, ps_r, ps_rs, cos_sb, sin_sb, t1, t2, res_r, res_rs, wr, scale=1.0, csres=("cs",)):
    S.op("dve", lambda e: e.tensor_tensor(t1, ps_r, cos_sb, ALU.mult), reads=[res_r, csres], writes=[("rt1",)])
    S.op("dve", lambda e: e.tensor_tensor(t2, ps_rs, sin_sb, ALU.mult), reads=[res_rs, csres], writes=[("rt2",)])
    if scale == 1.0:
        S.op("dve", lambda e: e.tensor_tensor(out_bf, t1, t2, ALU.add), reads=[("rt1",), ("rt2",)], writes=wr)
    else:
        S.op("dve", lambda e: e.scalar_tensor_tensor(t1, t1, 1.0, t2, ALU.mult, ALU.add), reads=[("rt1",), ("rt2",)], writes=[("rt1",)])
        S.op("dve", lambda e: e.tensor_scalar(out_bf, t1, scale, None, ALU.mult), reads=[("rt1",)], writes=wr)


def shared_kv(nc, S, C, hT, ntok, wdkv_t, wukv_k_t, wukv_v_t, g_in, g_kv, cosT, sinT, KT, Vtok, krT):
    with ExitStack() as st:
        ps = [st.enter_context(_pst(nc, "kv_ps%d" % i, [128, 512], F32)) for i in range(4)]
        psR = [st.enter_context(_pst(nc, "kv_psR%d" % i, [128, 512], F32)) for i in range(2)]
        psS = st.enter_context(_pst(nc, "kv_psS", [128, 512], F32))
        psS2 = st.enter_context(_pst(nc, "kv_psS2", [128, 512], F32))
        N = NormTiles(nc, st, "kv", psS, psS2, with_hb=False)
        hk2 = [st.enter_context(_sb(nc, "kv_hk%d" % i, [128, 16, 512], BF16)) for i in range(2)]
        wd = st.enter_context(_sb(nc, "kv_wd", [128, 16, 640], BF16))
        wk = st.enter_context(_sb(nc, "kv_wk", [128, 4, 2048], BF16))
        wv = st.enter_context(_sb(nc, "kv_wv", [128, 4, 2048], BF16))
        ck = st.enter_context(_sb(nc, "kv_ck", [128, 4, 512], F32))
        ckn = st.enter_context(_sb(nc, "kv_ckn", [128, 4, 512], BF16))
        cs2 = [st.enter_context(_sb(nc, "kv_cs%d" % i, [64, 2, 512], F32)) for i in range(2)]
        t1 = st.enter_context(_sb(nc, "kv_t1", [64, 512], F32))
        t2 = st.enter_context(_sb(nc, "kv_t2", [64, 512], F32))
        krs = st.enter_context(_sb(nc, "kv_kr", [64, 512], BF16))
        kst = [st.enter_context(_sb(nc, "kv_kst%d" % i, [128, 512], BF16)) for i in range(2)]
        vst = [st.enter_context(_sb(nc, "kv_vst%d" % i, [128, 2048], BF16)) for i in range(2)]
        S.dma("poolq", lambda e: e.dma_start(out=wd[:, :, :], in_=wdkv_t.rearrange("p (k n) -> p k n", k=16)), writes=[("wd",)])
        S.dma("poolq", lambda e: e.dma_start(out=wk[:, :, :], in_=wukv_k_t.rearrange("p (k n) -> p k n", k=4)), writes=[("wk",)])
        S.dma("poolq", lambda e: e.dma_start(out=wv[:, :, :], in_=wukv_v_t.rearrange("p (k n) -> p k n", k=4)), writes=[("wv",)])
        cnt = {"it": 0, "vi": 0}

        def stP(tb):
            ts = tb * 512
            par = tb % 2
            prenorm512(S, N, C, hT, ts, g_in, lambda c: (hk2[par][:, c, :], [("hk", par, c)]))
            S.dma("sp", lambda e, i=cosT[:, ts:ts + 512]: e.dma_start(out=cs2[par][:, 0, :], in_=i), writes=[("cs", par)])
            S.dma("sp", lambda e, i=sinT[:, ts:ts + 512]: e.dma_start(out=cs2[par][:, 1, :], in_=i), writes=[("cs", par)])

        def stM(tb):
            ts = tb * 512
            par = tb % 2
            hk = hk2[par]
            cs = cs2[par]
            it = cnt["it"]
            vi = cnt["vi"]
            for j in range(4):
                p = ps[it % 4]
                for kc in range(16):
                    _mm(S, p[:], wd[:, kc, j * 128:(j + 1) * 128], hk[:, kc, :], kc == 0, kc == 15,
                        reads=[("wd",), ("hk", par, kc)], writes=[("ps", it % 4)])
                S.op("dve", lambda e, o=ck[:, j, :], i=p[:]: e.tensor_copy(o, i), reads=[("ps", it % 4)], writes=[("ck", j)])
                y_stats(S, N, C, ck[:, j, :], [("ck", j)], j, 4)
                it += 1
            for q, col in ((0, 512), (1, 576)):
                for kc in range(16):
                    _mm(S, psR[q][0:64, :], wd[:, kc, col:col + 64], hk[:, kc, :], kc == 0, kc == 15,
                        reads=[("wd",), ("hk", par, kc)], writes=[("ps", "R", q)])
            rope_combine(S, krs[:, :], psR[0][0:64, :], psR[1][0:64, :], cs[:, 0, :], cs[:, 1, :], t1[:, :], t2[:, :],
                         ("ps", "R", 0), ("ps", "R", 1), [("krs",)], csres=("cs", par))
            S.dma("sp", lambda e, o=krT[:, ts:ts + 512]: e.dma_start(out=o, in_=krs[:, :]), reads=[("krs",)], writes=[("dram", "krT", tb)])
            post_rstd(S, N, 0, 512.0, 1.0)
            for j in range(4):
                S.op("dve", lambda e, o=ckn[:, j, :], i=ck[:, j, :], g=g_kv[:, j:j + 1]:
                     e.scalar_tensor_tensor(o, i, g, N.rstd_post[0][:], ALU.mult, ALU.mult),
                     reads=[("ck", j), ("nrstdQ", 0)], writes=[("ckn", j)])
            for hd in range(16):
                p = ps[it % 4]
                for j in range(4):
                    _mm(S, p[:], wk[:, j, hd * 128:(hd + 1) * 128], ckn[:, j, :], j == 0, j == 3,
                        reads=[("wk",), ("ckn", j)], writes=[("ps", it % 4)])
                ks = kst[hd % 2]
                if hd % 2 == 0:
                    _act(S, ks[:, :], p[:], AF.Copy, reads=[("ps", it % 4)], writes=[("kst", hd % 2)])
                else:
                    S.op("dve", lambda e, o=ks[:, :], i=p[:]: e.tensor_copy(o, i), reads=[("ps", it % 4)], writes=[("kst", hd % 2)])
                S.dma("sp", lambda e, o=KT[hd][:, ts:ts + 512], i=ks[:, :]: e.dma_start(out=o, in_=i),
                      reads=[("kst", hd % 2)], writes=[("dram", "KT", hd, tb)])
                it += 1
            for tt in range(4):
                vs = vst[vi % 2]
                for vb in range(4):
                    p = ps[it % 4]
                    for j in range(4):
                        _mm(S, p[:], ckn[:, j, tt * 128:(tt + 1) * 128], wv[:, j, vb * 512:(vb + 1) * 512], j == 0, j == 3,
                            reads=[("wv",), ("ckn", j)], writes=[("ps", it % 4)])
                    if vb % 2 == 0:
                        _act(S, vs[:, vb * 512:(vb + 1) * 512], p[:], AF.Copy, reads=[("ps", it % 4)], writes=[("vst", vi % 2)])
                    else:
                        S.op("dve", lambda e, o=vs[:, vb * 512:(vb + 1) * 512], i=p[:]: e.tensor_copy(o, i),
                             reads=[("ps", it % 4)], writes=[("vst", vi % 2)])
                    it += 1
                S.dma("sp", lambda e, o=Vtok[tb * 4 + tt], i=vs[:, :]: e.dma_start(out=o, in_=i),
                      reads=[("vst", vi % 2)], writes=[("dram", "Vtok", tb, tt)])
                vi += 1
            cnt["it"] = it
            cnt["vi"] = vi

        skew(ntok // 512, [stP, stM])
        S.flush()


MLA_SCALE = 192.0 ** -0.5


def mla_q(nc, S, C, h_in, ntok, tok_off, wdq_t, wuqn_t, wuqr_t, wuqrs_t, gpre, g_q, cosT, sinT, qnT, qrT):
    with ExitStack() as st:
        ps = [st.enter_context(_pst(nc, "mq_ps%d" % i, [128, 512], F32)) for i in range(2)]
        psR = [st.enter_context(_pst(nc, "mq_psR%d" % i, [128, 512], F32)) for i in range(4)]
        psS = st.enter_context(_pst(nc, "mq_psS", [128, 512], F32))
        psS2 = st.enter_context(_pst(nc, "mq_psS2", [128, 512], F32))
        N = NormTiles(nc, st, "mq", psS, psS2, with_hb=False)
        hn2 = [st.enter_context(_sb(nc, "mq_hn%d" % i, [128, 16, 512], BF16)) for i in range(2)]
        wd = st.enter_context(_sb(nc, "mq_wd", [128, 16, 512], BF16))
        wn = st.enter_context(_sb(nc, "mq_wn", [128, 4, 2048], BF16))
        wr = st.enter_context(_sb(nc, "mq_wr", [128, 4, 1024], BF16))
        wrs = st.enter_context(_sb(nc, "mq_wrs", [128, 4, 1024], BF16))
        cq = st.enter_context(_sb(nc, "mq_cq", [128, 4, 512], F32))
        cqn = st.enter_context(_sb(nc, "mq_cqn", [128, 4, 512], BF16))
        cs2 = [st.enter_context(_sb(nc, "mq_cs%d" % i, [64, 2, 512], F32)) for i in range(2)]
        t1 = st.enter_context(_sb(nc, "mq_t1", [64, 512], F32))
        t2 = st.enter_context(_sb(nc, "mq_t2", [64, 512], F32))
        qst = [st.enter_context(_sb(nc, "mq_qst%d" % i, [128, 512], BF16)) for i in range(2)]
        rst = [st.enter_context(_sb(nc, "mq_rst%d" % i, [64, 512], BF16)) for i in range(2)]
        S.dma("poolq", lambda e: e.dma_start(out=wd[:, :, :], in_=wdq_t.rearrange("p (k n) -> p k n", k=16)), writes=[("wd",)])
        S.dma("poolq", lambda e: e.dma_start(out=wn[:, :, :], in_=wuqn_t.rearrange("p (k n) -> p k n", k=4)), writes=[("wn",)])
        S.dma("poolq", lambda e: e.dma_start(out=wr[:, :, :], in_=wuqr_t.rearrange("p (k n) -> p k n", k=4)), writes=[("wr",)])
        S.dma("poolq", lambda e: e.dma_start(out=wrs[:, :, :], in_=wuqrs_t.rearrange("p (k n) -> p k n", k=4)), writes=[("wrs",)])
        cnt = {"it": 0, "ir": 0}

        def stP(tb):
            ts = tb * 512
            par = tb % 2
            prenorm512(S, N, C, h_in, ts, gpre, lambda c: (hn2[par][:, c, :], [("hn", par, c)]))
            S.dma("sp", lambda e, i=cosT[:, tok_off + ts:tok_off + ts + 512]: e.dma_start(out=cs2[par][:, 0, :], in_=i), writes=[("cs", par)])
            S.dma("sp", lambda e, i=sinT[:, tok_off + ts:tok_off + ts + 512]: e.dma_start(out=cs2[par][:, 1, :], in_=i), writes=[("cs", par)])

        def stM(tb):
            ts = tb * 512
            par = tb % 2
            hn = hn2[par]
            cs = cs2[par]
            it = cnt["it"]
            ir = cnt["ir"]
            for j in range(4):
                p = ps[it % 2]
                for kc in range(16):
                    _mm(S, p[:], wd[:, kc, j * 128:(j + 1) * 128], hn[:, kc, :], kc == 0, kc == 15,
                        reads=[("wd",), ("hn", par, kc)], writes=[("ps", it % 2)])
                S.op("dve", lambda e, o=cq[:, j, :], i=p[:]: e.tensor_copy(o, i), reads=[("ps", it % 2)], writes=[("cq", j)])
                y_stats(S, N, C, cq[:, j, :], [("cq", j)], j, 4)
                it += 1
            post_rstd(S, N, 0, 512.0, 1.0)
            for j in range(4):
                S.op("dve", lambda e, o=cqn[:, j, :], i=cq[:, j, :], g=g_q[:, j:j + 1]:
                     e.scalar_tensor_tensor(o, i, g, N.rstd_post[0][:], ALU.mult, ALU.mult),
                     reads=[("cq", j), ("nrstdQ", 0)], writes=[("cqn", j)])
            for hd in range(16):
                p = ps[it % 2]
                for j in range(4):
                    _mm(S, p[:], wn[:, j, hd * 128:(hd + 1) * 128], cqn[:, j, :], j == 0, j == 3,
                        reads=[("wn",), ("cqn", j)], writes=[("ps", it % 2)])
                qs = qst[hd % 2]
                _act(S, qs[:, :], p[:], AF.Copy, reads=[("ps", it % 2)], writes=[("qst", hd % 2)], scale=MLA_SCALE)
                S.dma("sp", lambda e, o=qnT[hd][:, ts:ts + 512], i=qs[:, :]: e.dma_start(out=o, in_=i),
                      reads=[("qst", hd % 2)], writes=[("dram", "qnT", hd, tb)])
                it += 1
                pr, prs = psR[2 * (ir % 2)], psR[2 * (ir % 2) + 1]
                for j in range(4):
                    _mm(S, pr[0:64, :], wr[:, j, hd * 64:(hd + 1) * 64], cqn[:, j, :], j == 0, j == 3,
                        reads=[("wr",), ("cqn", j)], writes=[("ps", "R", 2 * (ir % 2))])
                for j in range(4):
                    _mm(S, prs[0:64, :], wrs[:, j, hd * 64:(hd + 1) * 64], cqn[:, j, :], j == 0, j == 3,
                        reads=[("wrs",), ("cqn", j)], writes=[("ps", "R", 2 * (ir % 2) + 1)])
                rs = rst[hd % 2]
                rope_combine(S, rs[:, :], pr[0:64, :], prs[0:64, :], cs[:, 0, :], cs[:, 1, :], t1[:, :], t2[:, :],
                             ("ps", "R", 2 * (ir % 2)), ("ps", "R", 2 * (ir % 2) + 1), [("rst", hd % 2)], scale=MLA_SCALE, csres=("cs", par))
                S.dma("sp", lambda e, o=qrT[hd][:, ts:ts + 512], i=rs[:, :]: e.dma_start(out=o, in_=i),
                      reads=[("rst", hd % 2)], writes=[("dram", "qrT", hd, tb)])
                ir += 1
            cnt["it"] = it
            cnt["ir"] = ir

        skew(ntok // 512, [stP, stM])
        S.flush()


def mla_attn_old(nc, S, C, nq, nkeys, qnT, qrT, KT, Vtok, krT, cmask_d, pmask, oT):
    npk = (nkeys - nq) // 512
    NB = nkeys // 128
    with ExitStack() as st:
        psS = [st.enter_context(_pst(nc, "ma_psS%d" % i, [128, 512], F32)) for i in range(3)]
        psT = [st.enter_context(_pst(nc, "ma_psT%d" % i, [128, 8, 128], BF16)) for i in range(2)]
        psO_ = [st.enter_context(_pst(nc, "ma_psO%d" % i, [128, 512], F32)) for i in range(2)]
        psO = [t[:, 0:128] for t in psO_]
        psX_ = st.enter_context(_pst(nc, "ma_psX", [128, 1024], BF16))
        psX = psX_[:, 0:128]
        kt = [st.enter_context(_sb(nc, "ma_kt%d" % i, [128, nkeys], BF16)) for i in range(2)]
        vt = [st.enter_context(_sb(nc, "ma_vt%d" % i, [128, NB, 128], BF16)) for i in range(2)]
        qn = [st.enter_context(_sb(nc, "ma_qn%d" % i, [128, nq], BF16)) for i in range(2)]
        qr = [st.enter_context(_sb(nc, "ma_qr%d" % i, [64, nq], BF16)) for i in range(2)]
        kr = st.enter_context(_sb(nc, "ma_kr", [64, nkeys], BF16))
        cm = st.enter_context(_sb(nc, "ma_cm", [128, 4, 512], F32))
        Ssb = [st.enter_context(_sb(nc, "ma_S%d" % i, [128, nkeys], F32)) for i in range(2)]
        Pb = [st.enter_context(_sb(nc, "ma_P%d" % i, [128, nkeys], BF16)) for i in range(2)]
        PT = [st.enter_context(_sb(nc, "ma_PT%d" % i, [128, NB, 128], BF16)) for i in range(2)]
        sv = [st.enter_context(_sb(nc, "ma_sv%d" % i, [128, 8], F32)) for i in range(4)]
        ot = [st.enter_context(_sb(nc, "ma_ot%d" % i, [128, 128], BF16)) for i in range(2)]
        oTh = [st.enter_context(_sb(nc, "ma_oTh%d" % i, [128, nq], BF16)) for i in range(2)]
        ident = C["ident_bf"]
        S.dma("sp", lambda e: e.dma_start(out=kr[:, :], in_=krT[:, :]), reads=[("dram", "krT")], writes=[("kr",)])
        S.dma("sp", lambda e: e.dma_start(out=cm[:, :, :], in_=cmask_d[:, :, :]), writes=[("cm",)])
        cmb = st.enter_context(_sb(nc, "ma_cmb", [128, 4, 512], BF16))
        S.op("dve", lambda e: e.tensor_copy(cmb[:, :, :], cm[:, :, :]), reads=[("cm",)], writes=[("cmb",)])
        mb = [st.enter_context(_sb(nc, "ma_mb%d" % i, [128, 8], F32)) for i in range(4)]
        NQT = nq // 128
        units = [(hd, qt) for hd in range(16) for qt in range(NQT)]
        cnt = {"si": 0, "ti": 0}

        def loads(hd):
            k2 = hd % 2
            S.dma("sp", lambda e, o=kt[k2][:, :], i=KT[hd]: e.dma_start(out=o, in_=i), reads=[("dram", "KT")], writes=[("kt", k2)])
            S.dma("sp", lambda e, o=vt[k2][:, :, :], i=Vtok[:, :, hd * 128:(hd + 1) * 128].rearrange("b p c -> p b c"):
                  e.dma_start(out=o, in_=i), reads=[("dram", "Vtok")], writes=[("vt", k2)])
            S.dma("sp", lambda e, o=qn[k2][:, :], i=qnT[hd]: e.dma_start(out=o, in_=i), reads=[("dram", "qnT")], writes=[("qn", k2)])
            S.dma("sp", lambda e, o=qr[k2][:, :], i=qrT[hd]: e.dma_start(out=o, in_=i), reads=[("dram", "qrT")], writes=[("qr", k2)])

        def stA(ui):
            hd, qt = units[ui]
            k2, k, k4 = hd % 2, ui % 2, ui % 4
            if qt == 0 and hd == 0:
                loads(0)
            if qt == 4 and hd + 1 < 16:
                loads(hd + 1)
            nk = npk + qt // 4 + 1
            qs = slice(qt * 128, (qt + 1) * 128)
            for kb in range(nk):
                si = cnt["si"]
                p = psS[si % 3]
                ks = slice(kb * 512, (kb + 1) * 512)
                diag = (kb == nk - 1)
                _mm(S, p[:], qn[k2][:, qs], kt[k2][:, ks], True, False, reads=[("qn", k2), ("kt", k2)], writes=[("ps", "S", si % 3)])
                _mm(S, p[:], qr[k2][:, qs], kr[:, ks], False, not diag, reads=[("qr", k2), ("kr",)], writes=[("ps", "S", si % 3)])
                if diag:
                    _mm(S, p[:], ident[:], cmb[:, qt % 4, :], False, True, reads=[("cmb",)], writes=[("ps", "S", si % 3)])
                sc1 = pmask[:, 0:1] if kb < npk else 0.0
                S.op("dve", lambda e, o=Ssb[k][:, ks], i=p[:], s1=sc1, a=mb[k4][:, kb:kb + 1]:
                     e.tensor_scalar(o, i, s1, None, ALU.add, ALU.max, accum_out=a),
                     reads=[("ps", "S", si % 3)], writes=[("Ssb", k, kb), ("mb", k4)])
                cnt["si"] += 1

        def stB(ui):
            hd, qt = units[ui]
            k, k4 = ui % 2, ui % 4
            nk = npk + qt // 4 + 1
            W = nk * 512
            sres = [("Ssb", k, kb) for kb in range(nk)]
            s_ = sv[k4]
            S.op("dve", lambda e, o=s_[:, 1:2], i=mb[k4][:, 0:nk]: e.tensor_reduce(o, i, AX.X, ALU.max, negate=True),
                 reads=[("mb", k4)], writes=[("sv", k4)])
            _act(S, Pb[k][:, 0:W], Ssb[k][:, 0:W], AF.Exp, reads=sres + [("sv", k4)], writes=[("Pb", k), ("sv", k4)],
                 bias=s_[:, 1:2], accum_out=s_[:, 2:3])

        def stC(ui):
            hd, qt = units[ui]
            k = ui % 2
            nk = npk + qt // 4 + 1
            for b8 in range((nk * 4 + 7) // 8):
                ti = cnt["ti"]
                pT_ = psT[ti % 2]
                nb8 = min(8, nk * 4 - b8 * 8)
                for j in range(nb8):
                    blk = b8 * 8 + j
                    S.op("pe", lambda e, o=pT_[:, j, :], i=Pb[k][:, blk * 128:(blk + 1) * 128]: e.transpose(o, i, ident[:]),
                         reads=[("Pb", k)], writes=[("ps", "T", ti % 2)])
                if ti % 2 == 0:
                    _act(S, PT[k][:, b8 * 8:b8 * 8 + nb8, :], pT_[:, 0:nb8, :], AF.Copy, reads=[("ps", "T", ti % 2)], writes=[("PT", k, b8)])
                else:
                    S.op("dve", lambda e, o=PT[k][:, b8 * 8:b8 * 8 + nb8, :], i=pT_[:, 0:nb8, :]: e.tensor_copy(o, i),
                         reads=[("ps", "T", ti % 2)], writes=[("PT", k, b8)])
                cnt["ti"] += 1

        def stD(ui):
            hd, qt = units[ui]
            k2, k, k4 = hd % 2, ui % 2, ui % 4
            nk = npk + qt // 4 + 1
            qs = slice(qt * 128, (qt + 1) * 128)
            s_ = sv[k4]
            pO = psO[k]
            nblk = nk * 4
            for blk in range(nblk):
                _mm(S, pO, PT[k][:, blk, :], vt[k2][:, blk, :], blk == 0, blk == nblk - 1,
                    reads=[("PT", k, blk // 8), ("vt", k2)], writes=[("ps", "O", k)])
            S.op("dve", lambda e, o=s_[:, 3:4], i=s_[:, 2:3]: e.reciprocal(o, i), reads=[("sv", k4)], writes=[("sv", k4)])
            S.op("dve", lambda e, o=ot[k][:, :], i=pO, r=s_[:, 3:4]: e.tensor_scalar(o, i, r, None, ALU.mult),
                 reads=[("ps", "O", k), ("sv", k4)], writes=[("ot", k)])
            S.op("pe", lambda e, i=ot[k][:, :]: e.transpose(psX, i, ident[:]), reads=[("ot", k)], writes=[("ps", "X")])
            _act(S, oTh[k2][:, qs], psX, AF.Copy, reads=[("ps", "X")], writes=[("oTh", k2)])
            if qt == NQT - 1:
                S.dma("sp", lambda e, o=oT[hd], i=oTh[k2][:, :]: e.dma_start(out=o, in_=i), reads=[("oTh", k2)], writes=[("dram", "oT", hd)])

        nU = len(units)
        for t in range(nU + 3):
            if t < nU:
                stA(t)
            if 0 <= t - 1 < nU:
                stB(t - 1)
            if 0 <= t - 2 < nU:
                stC(t - 2)
            if 0 <= t - 3 < nU:
                stD(t - 3)
        S.flush()


def mla_attn(nc, S, C, nq, nkeys, qnT, qrT, KT, Vtok, krT, cmT_d, pmask, oT):
    npb = (nkeys - nq) // 128
    NB = nkeys // 128
    NQG = nq // 512
    with ExitStack() as st:
        psS = [st.enter_context(_pst(nc, "mb_psS%d" % i, [128, 512], F32)) for i in range(3)]
        psO = [st.enter_context(_pst(nc, "mb_psO%d" % i, [128, 512], F32)) for i in range(2)]
        psL = [st.enter_context(_pst(nc, "mb_psL%d" % i, [128, 512], F32)) for i in range(2)]
        psZ = st.enter_context(_pst(nc, "mb_psZ", [128, 512], F32))
        kt = [st.enter_context(_sb(nc, "mb_kt%d" % i, [128, nkeys], BF16)) for i in range(2)]
        vt = [st.enter_context(_sb(nc, "mb_vt%d" % i, [128, NB, 128], BF16)) for i in range(2)]
        qn = [st.enter_context(_sb(nc, "mb_qn%d" % i, [128, nq], BF16)) for i in range(2)]
        qr = [st.enter_context(_sb(nc, "mb_qr%d" % i, [65, nq], BF16)) for i in range(2)]
        kr = st.enter_context(_sb(nc, "mb_kr", [65, nkeys], BF16))
        cm = st.enter_context(_sb(nc, "mb_cm", [128, 4, 512], BF16))
        sel = st.enter_context(_sb(nc, "mb_sel", [128, 65], BF16))
        ktsq = st.enter_context(_sb(nc, "mb_ktsq", [128, nkeys], BF16))
        krsq = st.enter_context(_sb(nc, "mb_krsq", [64, nkeys], BF16))
        qnsq = st.enter_context(_sb(nc, "mb_qnsq", [128, nq], BF16))
        qrsq = st.enter_context(_sb(nc, "mb_qrsq", [64, nq], BF16))
        rowq = st.enter_context(_sb(nc, "mb_rowq", [65, nq], F32))
        kb8 = st.enter_context(_sb(nc, "mb_kb8", [65, 16], F32))
        PTb = [st.enter_context(_sb(nc, "mb_PT%d" % i, [128, 512], BF16)) for i in range(4)]
        rl = [st.enter_context(_sb(nc, "mb_rl%d" % i, [128, 512], F32)) for i in range(2)]
        oTh = [st.enter_context(_sb(nc, "mb_oTh%d" % i, [128, nq], BF16)) for i in range(2)]
        ident = C["ident_bf"]
        ones = C["ones_bf"]
        S.dma("sp", lambda e: e.dma_start(out=kr[0:64, :], in_=krT[:, :]), reads=[("dram", "krT")], writes=[("kr",)])
        S.op("dve", lambda e: e.memset(kr[64:65, :], 1.0), writes=[("kr1",)])
        S.dma("poolq", lambda e: e.dma_start(out=cm[:, :, :], in_=cmT_d[:, :, :]), writes=[("cm",)])
        S.op("dve", lambda e: e.memset(sel[:, :], 0.0), writes=[("sel",)])
        S.op("dve", lambda e: e.memset(sel[:, 64:65], 1.0), reads=[("sel",)], writes=[("sel",)])
        _act(S, krsq[:, :], kr[0:64, :], AF.Square, reads=[("kr",)], writes=[("krsq",)])

        def loads(hd):
            k2 = hd % 2
            S.dma("sp", lambda e, o=kt[k2][:, :], i=KT[hd]: e.dma_start(out=o, in_=i), reads=[("dram", "KT")], writes=[("kt", k2)])
            S.dma("sp", lambda e, o=vt[k2][:, :, :], i=Vtok[:, :, hd * 128:(hd + 1) * 128].rearrange("b p c -> p b c"):
                  e.dma_start(out=o, in_=i), reads=[("dram", "Vtok")], writes=[("vt", k2)])
            S.dma("sp", lambda e, o=qn[k2][:, :], i=qnT[hd]: e.dma_start(out=o, in_=i), reads=[("dram", "qnT")], writes=[("qn", k2)])
            S.dma("sp", lambda e, o=qr[k2][0:64, :], i=qrT[hd]: e.dma_start(out=o, in_=i), reads=[("dram", "qrT")], writes=[("qr", k2)])

        def stabiliser(hd):
            k2 = hd % 2
            _act(S, ktsq[:, :], kt[k2][:, :], AF.Square, reads=[("kt", k2)], writes=[("ktsq",)])
            _act(S, qnsq[:, :], qn[k2][:, :], AF.Square, reads=[("qn", k2)], writes=[("qnsq",)])
            _act(S, qrsq[:, :], qr[k2][0:64, :], AF.Square, reads=[("qr", k2)], writes=[("qrsq",)])
            for j in range(nkeys // 512):
                ks = slice(j * 512, (j + 1) * 512)
                _mm(S, psZ[0:65, :], sel[:, :], ktsq[:, ks], True, False, reads=[("sel",), ("ktsq",)], writes=[("ps", "Z")])
                _mm(S, psZ[0:65, :], sel[0:64, :], krsq[:, ks], False, True, reads=[("sel",), ("krsq",)], writes=[("ps", "Z")])
                S.op("dve", lambda e, o=kb8[64:65, j:j + 1], i=psZ[64:65, :]: e.tensor_reduce(o, i, AX.X, ALU.max),
                     reads=[("ps", "Z")], writes=[("kb8",)])
            S.op("dve", lambda e: e.tensor_reduce(kb8[64:65, 15:16], kb8[64:65, 0:nkeys // 512], AX.X, ALU.max),
                 reads=[("kb8",)], writes=[("kb8",)])
            for j in range(nq // 512):
                qs = slice(j * 512, (j + 1) * 512)
                _mm(S, psZ[0:65, :], sel[:, :], qnsq[:, qs], True, False, reads=[("sel",), ("qnsq",)], writes=[("ps", "Z")])
                _mm(S, psZ[0:65, :], sel[0:64, :], qrsq[:, qs], False, True, reads=[("sel",), ("qrsq",)], writes=[("ps", "Z")])
                _act(S, rowq[64:65, qs], psZ[64:65, :], AF.Sqrt, reads=[("ps", "Z"), ("kb8",)], writes=[("rowq", j)],
                     scale=kb8[64:65, 15:16])
                S.op("dve", lambda e, o=qr[k2][64:65, qs], i=rowq[64:65, qs]: e.tensor_scalar(o, i, -1.02, None, ALU.mult),
                     reads=[("rowq", j)], writes=[("qrow", k2)])

        items = []
        for hd in range(16):
            for qg in range(NQG):
                nkb = npb + 4 * (qg + 1)
                for kb in range(nkb):
                    items.append((hd, qg, kb, nkb))

        def stA(i):
            hd, qg, kb, nkb = items[i]
            k2 = hd % 2
            if qg == 0 and kb == 0:
                if hd == 0:
                    loads(0)
                stabiliser(hd)
            if qg == 1 and kb == 0 and hd + 1 < 16:
                loads(hd + 1)
            p = psS[i % 3]
            qs = slice(qg * 512, (qg + 1) * 512)
            ks = slice(kb * 128, (kb + 1) * 128)
            v = kb - npb - 4 * qg
            _mm(S, p[:], kt[k2][:, ks], qn[k2][:, qs], True, False, reads=[("kt", k2), ("qn", k2)], writes=[("ps", "S", i % 3)])
            _mm(S, p[:], kr[0:65, ks], qr[k2][0:65, qs], False, v < 0, reads=[("kr",), ("kr1",), ("qr", k2), ("qrow", k2)],
                writes=[("ps", "S", i % 3)])
            if v >= 0:
                _mm(S, p[:], ident[:], cm[:, v, :], False, True, reads=[("cm",)], writes=[("ps", "S", i % 3)])

        def stB(i):
            hd, qg, kb, nkb = items[i]
            p = psS[i % 3]
            if kb < npb:
                _act(S, PTb[i % 4][:, :], p[:], AF.Exp, reads=[("ps", "S", i % 3)], writes=[("PTb", i % 4)], bias=pmask[:, 0:1])
            else:
                _act(S, PTb[i % 4][:, :], p[:], AF.Exp, reads=[("ps", "S", i % 3)], writes=[("PTb", i % 4)])

        def stC(i):
            hd, qg, kb, nkb = items[i]
            k2 = hd % 2
            g2 = (hd * NQG + qg) % 2
            _mm(S, psO[g2][:], vt[k2][:, kb, :], PTb[i % 4][:, :], kb == 0, kb == nkb - 1,
                reads=[("vt", k2), ("PTb", i % 4)], writes=[("ps", "O", g2)])
            _mm(S, psL[g2][:], ones[:], PTb[i % 4][:, :], kb == 0, kb == nkb - 1,
                reads=[("PTb", i % 4)], writes=[("ps", "L", g2)])
            if kb == nkb - 1:
                qs = slice(qg * 512, (qg + 1) * 512)
                S.op("dve", lambda e, o=rl[g2][:, :], i=psL[g2][:]: e.reciprocal(o, i), reads=[("ps", "L", g2)], writes=[("rl", g2)])
                S.op("dve", lambda e, o=oTh[k2][:, qs], a=psO[g2][:], b=rl[g2][:, :]: e.tensor_tensor(o, a, b, ALU.mult),
                     reads=[("ps", "O", g2), ("rl", g2)], writes=[("oTh", k2)])
                if qg == NQG - 1:
                    S.dma("sp", lambda e, o=oT[hd], i=oTh[k2][:, :]: e.dma_start(out=o, in_=i),
                          reads=[("oTh", k2)], writes=[("dram", "oT", hd)])

        skew(len(items), [stA, stB, stC])
        S.flush()


def proj_norm_residual(nc, S, C, h_in, h_out, ntok, xT, nk, w_t, gain, pfx):
    with ExitStack() as st:
        ps = [st.enter_context(_pst(nc, pfx + "_ps%d" % i, [128, 512], F32)) for i in range(2)]
        psS = st.enter_context(_pst(nc, pfx + "_psS", [128, 512], F32))
        N = NormTiles(nc, st, pfx, psS, psS, with_hs=False)
        xb = [st.enter_context(_sb(nc, pfx + "_xb%d" % i, [128, nk, 512], BF16)) for i in range(2)]
        w = [st.enter_context(_sb(nc, pfx + "_w%d" % i, [128, nk * 128], BF16)) for i in range(2)]
        Ysb = [st.enter_context(_sb(nc, pfx + "_Y%d" % i, [128, 16, 512], F32)) for i in range(2)]

        def stM(tb):
            ts = tb * 512
            x = xb[tb % 2]
            Y = Ysb[tb % 2]
            S.dma("sp", lambda e, o=x[:, :, :], i=xT[:, :, ts:ts + 512].rearrange("k p t -> p k t"): e.dma_start(out=o, in_=i),
                  reads=[("dram", "xT")], writes=[("xb", tb % 2)])
            for c in range(16):
                k = c % 2
                S.dma("poolq", lambda e, o=w[k][:, :], i=w_t[c]: e.dma_start(out=o, in_=i), writes=[("w", k)])
                for kc in range(nk):
                    _mm(S, ps[k][:], w[k][:, kc * 128:(kc + 1) * 128], x[:, kc, :], kc == 0, kc == nk - 1,
                        reads=[("w", k), ("xb", tb % 2)], writes=[("ps", k)])
                S.op("dve", lambda e, o=Y[:, c, :], i=ps[k][:]: e.tensor_copy(o, i), reads=[("ps", k)], writes=[("Y", tb % 2, c)])
                y_stats(S, N, C, Y[:, c, :], [("Y", tb % 2, c)], c, 16)
            post_rstd(S, N, tb % 2, float(D), 1.0)

        def stR(tb):
            postnorm_residual512(S, N, tb % 2, Ysb[tb % 2], lambda c: [("Y", tb % 2, c)], gain, h_in, h_out, tb * 512)

        skew(ntok // 512, [stM, stR])
        S.flush()


NG = 288

W_SPECS = [
    ("f1a_gu", [FC, 128, 4096]), ("f1a_d", [16, 128, 5632]), ("f2a_gu", [FC, 128, 4096]), ("f2a_d", [16, 128, 5632]),
    ("f1b_gu", [FC, 128, 4096]), ("f1b_d", [16, 128, 5632]), ("f2b_gu", [FC, 128, 4096]), ("f2b_d", [16, 128, 5632]),
    ("pg0", [16, 128, 2048]), ("pp0", [128, 4096]), ("pg1", [16, 128, 2048]), ("pp1", [128, 4096]),
    ("wqkv", [72, 128, 2048]), ("awo", [128, 16384]),
    ("wdkv", [128, 16 * 640]), ("wk", [128, 4 * 2048]), ("wv", [128, 4 * 2048]),
    ("wdq", [128, 16 * 512]), ("wn", [128, 4 * 2048]), ("wr", [128, 4 * 1024]), ("wrs", [128, 4 * 1024]),
    ("bwo", [16, 128, 2048]),
]


def build_program():
    nc = bass.Bass("TRN2", target_bir_lowering=False)
    NT, NQ = SEQ, HALF
    dt = lambda n, s, d=F32: nc.dram_tensor(n, s, d, kind="ExternalInput").ap()
    xT = dt("xT", [16, 128, NT])
    p0T = dt("p0T", [2, 128, NT])
    p1T = dt("p1T", [2, 128, NQ])
    pos = dt("pos", [1, NT], I32)
    pm = dt("pm", [128, 1])
    gn = dt("gn", [128, NG])
    identd = dt("identd", [128, 128])
    ropec_d = dt("ropec", [64, 4])
    cmd = dt("cmask", [128, 4, 512])
    alibi = dt("alibi", [24, 128, 256])
    W = {n: dt(n, s) for n, s in W_SPECS}
    outT = nc.dram_tensor("outT", [16, 128, NQ], F32, kind="ExternalOutput").ap()
    hT = nc.dram_tensor("hT", [16, 128, NT], F32).ap()
    qkvT = nc.dram_tensor("qkvT", [72, 128, NT], BF16).ap()
    Og = nc.dram_tensor("Og", [3, NT, 1032], F32).ap()
    cosT = nc.dram_tensor("cosT", [64, NT], F32).ap()
    sinT = nc.dram_tensor("sinT", [64, NT], F32).ap()
    KT = nc.dram_tensor("KT", [16, 128, NT], BF16).ap()
    Vtok = nc.dram_tensor("Vtok", [NT // 128, 128, 2048], BF16).ap()
    krT = nc.dram_tensor("krT", [64, NT], BF16).ap()
    qnT = nc.dram_tensor("qnT", [16, 128, NQ], BF16).ap()
    qrT = nc.dram_tensor("qrT", [16, 64, NQ], BF16).ap()
    oT = nc.dram_tensor("oT", [16, 128, NQ], BF16).ap()
    with ExitStack() as st:
        S = Sched(nc, st)
        ones = st.enter_context(_sb(nc, "ones", [128, 128], BF16))
        ident = st.enter_context(_sb(nc, "ident", [128, 128], BF16))
        g = st.enter_context(_sb(nc, "g", [128, NG], F32))
        pmask = st.enter_context(_sb(nc, "pmask", [128, 1], F32))
        ropec = st.enter_context(_sb(nc, "ropec_sb", [64, 4], F32))
        S.op("dve", lambda e: e.memset(ones[:], 1.0), writes=[("ones",)])
        S.dma("sp", lambda e: e.dma_start(out=g[:], in_=gn[:, :]), writes=[("g",)])
        S.dma("sp", lambda e: e.dma_start(out=pmask[:], in_=pm[:, :]), writes=[("pmk",)])
        S.dma("sp", lambda e: e.dma_start(out=ropec[:], in_=ropec_d[:, :]), writes=[("rc",)])
        S.dma("poolq", lambda e: e.dma_start(out=ident[:], in_=identd[:, :]), writes=[("id",)])
        S.flush()
        C = {"ones_bf": ones, "ident_bf": ident}
        G = lambda l, n: g[:, (l * 8 + n) * 16:(l * 8 + n + 1) * 16]
        g_kvin, g_kv, g_q = g[:, 256:272], g[:, 272:276], g[:, 276:280]

        def ffn(h_in, h_out, ntok, wgu, wd, gpre, gpost):
            with ExitStack() as st2:
                Tl = FFNTiles(nc, st2)
                ffn_block(nc, S, Tl, C, h_in, h_out, ntok, wgu, wd, gpre, gpost)
                S.flush()

        rope_tables(nc, S, C, pos, NT, ropec, cosT, sinT)
        ffn(xT, hT, NT, W["f1a_gu"], W["f1a_d"], G(0, 0), G(0, 1))
        mixA_qkv(nc, S, C, hT, NT, W["wqkv"], G(0, 2), qkvT)
        mixA_attn(nc, S, C, NT, qkvT, alibi, pmask, Og)
        mixA_out(nc, S, C, hT, NT, Og, W["awo"], G(0, 3))
        ffn(hT, hT, NT, W["f2a_gu"], W["f2a_d"], G(0, 4), G(0, 5))
        ple_block(nc, S, C, hT, hT, NT, p0T, W["pg0"], W["pp0"], G(0, 6), G(0, 7))
        shared_kv(nc, S, C, hT, NT, W["wdkv"], W["wk"], W["wv"], g_kvin, g_kv, cosT, sinT, KT, Vtok, krT)
        hO = hT[:, :, NT - NQ:NT]
        ffn(hO, hO, NQ, W["f1b_gu"], W["f1b_d"], G(1, 0), G(1, 1))
        mla_q(nc, S, C, hO, NQ, NT - NQ, W["wdq"], W["wn"], W["wr"], W["wrs"], G(1, 2), g_q, cosT, sinT, qnT, qrT)
        mla_attn(nc, S, C, NQ, NT, qnT, qrT, KT, Vtok, krT, cmd, pmask, oT)
        proj_norm_residual(nc, S, C, hO, hO, NQ, oT, 16, W["bwo"], G(1, 3), "mo")
        ffn(hO, hO, NQ, W["f2b_gu"], W["f2b_d"], G(1, 4), G(1, 5))
        ple_block(nc, S, C, hO, outT, NQ, p1T, W["pg1"], W["pp1"], G(1, 6), G(1, 7))
    return nc


def _tile_cols(w, width=128):
    K, Nc = w.shape
    return np.ascontiguousarray(w.reshape(K // 128, 128, Nc // width, width).transpose(2, 1, 0, 3)).reshape(
        Nc // width, 128, (K // 128) * width)


def _tile_rows(w):
    K, Nc = w.shape
    return np.ascontiguousarray(w.reshape(K // 128, 128, Nc).transpose(1, 0, 2)).reshape(128, (K // 128) * Nc)


def _tile_wgu(wg, wu):
    a = wg.reshape(16, 128, FC, 128).transpose(2, 1, 0, 3)
    b = wu.reshape(16, 128, FC, 128).transpose(2, 1, 0, 3)
    return np.ascontiguousarray(np.stack([a, b], axis=2)).reshape(FC, 128, 4096)


def _tile_wd(wd):
    return np.ascontiguousarray(wd.reshape(FC, 128, 16, 128).transpose(2, 1, 0, 3)).reshape(16, 128, 5632)


def _gl(gv):
    return np.ascontiguousarray(np.asarray(gv, np.float32).reshape(-1, 128).T)


def _fm(a):
    t, f = a.shape
    return np.ascontiguousarray(a.T).reshape(f // 128, 128, t)


def _const_tables():
    slopes = 2.0 ** (-8.0 * np.arange(1, 25) / 24)
    q = np.arange(128)[:, None]
    k = np.arange(256)[None, :]
    diff = 128 + q - k
    valid = (diff >= 0) & (diff <= 128)
    alibi = np.zeros((24, 128, 256), np.float32)
    for gi in range(3):
        for h in range(8):
            alibi[gi * 8 + h] = np.where(valid, -(slopes[gi * 8 + h] * A_DIL[gi]) * diff, NEGB)
    pidx = np.arange(64)
    ropec = np.stack([10000.0 ** (-(2.0 * (pidx % 32)) / 64), np.where(pidx < 32, -1.0, 1.0),
                      np.full(64, -np.pi), np.zeros(64)], axis=1).astype(np.float32)
    q_ = np.arange(128)[:, None, None]
    v_ = np.arange(4)[None, :, None]
    kk = np.arange(512)[None, None, :]
    cmask = np.where(kk >= v_ * 128 + q_, 0.0, NEGB).astype(np.float32)
    return alibi, ropec, cmask


def prepare_inputs(x, p, positions, norms, ffn1_wg, ffn1_wu, ffn1_wd, ffn2_wg, ffn2_wu, ffn2_wd,
                   ple_proj, ple_gate, a_wqkv, a_wo, b_wdq, b_q_norm, b_wuq, b_wo,
                   kv_in_norm, w_dkv, kv_norm, w_ukv, cores=range(8)):
    f32 = lambda a: np.asarray(a, np.float32)
    x, p = f32(x), f32(p)
    positions = np.asarray(positions, np.int32)
    norms = f32(norms)
    alibi, ropec, cmask = _const_tables()
    shared = {"identd": np.eye(128, dtype=np.float32), "ropec": ropec, "cmask": cmask, "alibi": alibi}
    gcols = [_gl(norms[l, n]) for l in range(2) for n in range(8)]
    gcols += [_gl(kv_in_norm), _gl(kv_norm), _gl(b_q_norm), np.zeros((128, NG - 280), np.float32)]
    shared["gn"] = np.ascontiguousarray(np.concatenate(gcols, axis=1))
    shared["f1a_gu"] = _tile_wgu(f32(ffn1_wg[0]), f32(ffn1_wu[0])); shared["f1a_d"] = _tile_wd(f32(ffn1_wd[0]))
    shared["f2a_gu"] = _tile_wgu(f32(ffn2_wg[0]), f32(ffn2_wu[0])); shared["f2a_d"] = _tile_wd(f32(ffn2_wd[0]))
    shared["f1b_gu"] = _tile_wgu(f32(ffn1_wg[1]), f32(ffn1_wu[1])); shared["f1b_d"] = _tile_wd(f32(ffn1_wd[1]))
    shared["f2b_gu"] = _tile_wgu(f32(ffn2_wg[1]), f32(ffn2_wu[1])); shared["f2b_d"] = _tile_wd(f32(ffn2_wd[1]))
    for l in range(2):
        shared["pg%d" % l] = _tile_cols(f32(ple_gate[l]))
        shared["pp%d" % l] = _tile_rows(f32(ple_proj[l]))
    shared["wqkv"] = _tile_cols(f32(a_wqkv[0]))
    shared["awo"] = np.ascontiguousarray(f32(a_wo[0]).reshape(8, 128, 16, 128).transpose(1, 2, 0, 3)).reshape(128, 16384)
    wd_ = f32(w_dkv)
    rp = wd_[:, 512:576]
    shared["wdkv"] = _tile_rows(np.concatenate([wd_, rp[:, 32:], rp[:, :32]], axis=1))
    wu = f32(w_ukv).reshape(512, 16, 256)
    shared["wk"] = _tile_rows(np.ascontiguousarray(wu[:, :, :128]).reshape(512, 2048))
    shared["wv"] = _tile_rows(np.ascontiguousarray(wu[:, :, 128:]).reshape(512, 2048))
    shared["wdq"] = _tile_rows(f32(b_wdq[0]))
    wq = f32(b_wuq[0]).reshape(512, 16, 192)
    wqr = wq[:, :, 128:]
    wqrs = np.concatenate([wqr[:, :, 32:], wqr[:, :, :32]], axis=2)
    shared["wn"] = _tile_rows(np.ascontiguousarray(wq[:, :, :128]).reshape(512, 2048))
    shared["wr"] = _tile_rows(np.ascontiguousarray(wqr).reshape(512, 1024))
    shared["wrs"] = _tile_rows(np.ascontiguousarray(wqrs).reshape(512, 1024))
    shared["bwo"] = _tile_cols(f32(b_wo[0]))
    in_maps = []
    for c in cores:
        b, half = c // 2, c % 2
        sel = np.concatenate([np.arange(0, HALF), np.arange(half * HALF, (half + 1) * HALF)])
        m = dict(shared)
        m["xT"] = _fm(x[b][sel])
        m["p0T"] = _fm(p[0, b][sel])
        m["p1T"] = _fm(p[1, b][half * HALF:(half + 1) * HALF])
        m["pos"] = np.ascontiguousarray(positions[b][sel][None, :])
        m["pm"] = np.full((128, 1), 0.0 if half == 1 else NEGB, np.float32)
        in_maps.append(m)
    return in_maps


def kernel(**inputs):
    in_maps = prepare_inputs(**inputs)
    nc = build_program()
    res = run_bass_kernel_spmd(nc, in_maps, core_ids=list(range(8)))
    out = np.empty((4, SEQ, D), np.float32)
    for c in range(8):
        b, half = c // 2, c % 2
        oT_ = np.asarray(res.results[c]["outT"], np.float32).reshape(D, HALF)
        out[b, half * HALF:(half + 1) * HALF, :] = oT_.T
    return out
```
